# Optimizing a Trainium2 kernel written in Bass

```python
import jax, jax.numpy as jnp
from jax import lax
import numpy as np

D_MODEL = 1024
BATCH = 2
SEQ = 8192
DEPTH = 1
DEC_BATCH = 32
DEC_SEQ = 4
PAST_LEN = 16384
PAGE_SIZE = 128

A_WIDTH = D_MODEL // 2
A_GROUPS = 8
A_GDIM = A_WIDTH // A_GROUPS
CHUNK = 128
N_HEADS = 8
HEAD_DIM = 64
N_KV = 2
GROUP = N_HEADS // N_KV
B_WIDTH = N_HEADS * HEAD_DIM
KV_WIDTH = N_KV * HEAD_DIM
CMP_BLOCK = 32
CMP_STRIDE = 16
SEL_BLOCK = 64
N_SELECT = 16
WINDOW = 512
ROT_DIM = HEAD_DIM // 4
ROPE_THETA = 500000.0
D_FF = 4 * D_MODEL
IN_COLS = 2 * A_WIDTH + B_WIDTH + 6 * KV_WIDTH + 3 * N_HEADS + 2 * D_MODEL
EPS = 1e-6
Q_BLOCK = 128
NEG = -1e30
FORCED_BONUS = 1e4

kernel_name = 'hybrid_gmlp_nsa_decode_step'


def rms_norm(x, g):
    xf = x.astype(jnp.float32)
    y = xf * lax.rsqrt(jnp.mean(xf * xf, axis=-1, keepdims=True) + EPS)
    return (y * g.astype(jnp.float32)).astype(x.dtype)


def layer_norm(x, g, b):
    xf = x.astype(jnp.float32)
    xc = xf - jnp.mean(xf, axis=-1, keepdims=True)
    y = xc * lax.rsqrt(jnp.mean(xc * xc, axis=-1, keepdims=True) + EPS)
    return (y * g.astype(jnp.float32) + b.astype(jnp.float32)).astype(x.dtype)


def rope(x, pos):
    inv = ROPE_THETA ** (-jnp.arange(0, ROT_DIM, 2, dtype=jnp.float32) / ROT_DIM)
    ang = pos.astype(jnp.float32)[:, None] * inv[None, :]
    cos = jnp.cos(ang)[None, :, None, :]
    sin = jnp.sin(ang)[None, :, None, :]
    xr = x[..., :ROT_DIM].astype(jnp.float32)
    x1, x2 = xr[..., :ROT_DIM // 2], xr[..., ROT_DIM // 2:]
    rot = jnp.concatenate([x1 * cos - x2 * sin, x2 * cos + x1 * sin], axis=-1)
    return jnp.concatenate([rot.astype(x.dtype), x[..., ROT_DIM:]], axis=-1)


def masked_softmax(s, mask):
    s = jnp.where(mask, s.astype(jnp.float32), NEG)
    return jax.nn.softmax(s, axis=-1) * mask


def in_projection(x, pos, g_norm1, w_in, ln_v_g, ln_v_b, g_q, g_ks, g_kw):
    B, T, _ = x.shape
    h = rms_norm(x, g_norm1)
    z = h @ w_in
    widths = [A_WIDTH, A_WIDTH, B_WIDTH] + [KV_WIDTH] * 6 + [3 * N_HEADS]
    points, acc = [], 0
    for w in widths:
        acc += w
        points.append(acc)
    u, v, q, kc, vc, ks, vs, kw, vw, nsa_gate, merge_gate = jnp.split(z, points, axis=-1)
    u = jax.nn.gelu(u)
    v_n = layer_norm(jax.nn.gelu(v), ln_v_g, ln_v_b)
    heads = lambda a, n: a.reshape(B, T, n, HEAD_DIM)
    q_n = rms_norm(heads(q, N_HEADS), g_q)
    q_r = rope(q_n, pos)
    ks = rope(rms_norm(heads(ks, N_KV), g_ks), pos)
    kw = rope(rms_norm(heads(kw, N_KV), g_kw), pos)
    return (u, v_n, q_n, q_r, heads(kc, N_KV), heads(vc, N_KV), ks, heads(vs, N_KV),
            kw, heads(vw, N_KV), nsa_gate, merge_gate)


def gmlp_mix(u, v_n, w_s, b_s):
    B, T, _ = v_n.shape
    n_chunks = -(-T // CHUNK)
    pad = n_chunks * CHUNK - T
    vp = jnp.pad(v_n, ((0, 0), (0, pad), (0, 0))).reshape(B, n_chunks, CHUNK, A_GROUPS, A_GDIM)
    causal = jnp.tril(jnp.ones((CHUNK, CHUNK), dtype=bool))
    ws = jnp.where(causal[None], w_s, jnp.zeros_like(w_s))
    s = jnp.einsum('gpr,bnrgc->bnpgc', ws, vp) + b_s.T[None, None, :, :, None]
    return u * s.reshape(B, n_chunks * CHUNK, A_WIDTH)[:, :T]


def compress(rows, w1, w2, pe):
    B, T, KV, hd = rows.shape
    C = T // CMP_STRIDE
    half = CMP_STRIDE * hd
    ch = rows[:, :C * CMP_STRIDE].reshape(B, C, CMP_STRIDE, KV, hd).transpose(0, 1, 3, 2, 4).reshape(B, C, KV, half)
    first = ch @ w1[:half]
    second = ch @ w1[half:]
    bias = pe.reshape(-1) @ w1
    return jax.nn.gelu(first[:, :-1] + second[:, 1:] + bias) @ w2


def nsa_attend(q_n, q_r, gate, pos_q, kc, vc, fetch_sel, k_w, v_w, pos_w, n_total):
    B, Tq = q_n.shape[:2]
    scale = HEAD_DIM ** -0.5
    qn = q_n.reshape(B, Tq, N_KV, GROUP, HEAD_DIM)
    qr = q_r.reshape(B, Tq, N_KV, GROUP, HEAD_DIM)
    NC = kc.shape[1]
    cmp_end = jnp.arange(NC) * CMP_STRIDE + CMP_BLOCK - 1
    m_c = cmp_end[None, :] <= pos_q[:, None]
    p_c = masked_softmax(jnp.einsum('bqkgd,bckd->bkgqc', qn, kc) * scale, m_c)
    o_c = jnp.einsum('bkgqc,bckd->bqkgd', p_c.astype(vc.dtype), vc)
    imp = p_c.sum(axis=2)
    n_sel = -(-n_total // SEL_BLOCK)
    n_ch = n_sel * (SEL_BLOCK // CMP_STRIDE)
    chunk_score = (jnp.pad(imp, ((0, 0), (0, 0), (0, 0), (0, n_ch - NC)))
                   + jnp.pad(imp, ((0, 0), (0, 0), (0, 0), (1, n_ch - NC - 1))))
    blk_score = chunk_score.reshape(B, N_KV, Tq, n_sel, SEL_BLOCK // CMP_STRIDE).sum(-1)
    j = jnp.arange(n_sel)[None, :]
    cur = (pos_q // SEL_BLOCK)[:, None]
    valid_b = j * SEL_BLOCK <= pos_q[:, None]
    forced = (j == 0) | (j == cur) | (j == cur - 1)
    rank = jnp.where(forced, FORCED_BONUS, jnp.where(valid_b, blk_score, -FORCED_BONUS))
    n_top = min(N_SELECT, n_sel)
    _, idx = lax.top_k(rank, n_top)
    pq = pos_q[None, None, :, None, None]
    pos_s = idx[..., None] * SEL_BLOCK + jnp.arange(SEL_BLOCK)
    m_s = ((idx[..., None] * SEL_BLOCK <= pq) & (pos_s <= pq)).reshape(B, N_KV, Tq, n_top * SEL_BLOCK)
    k_s, v_s = fetch_sel(pos_s.reshape(B, N_KV, Tq, n_top * SEL_BLOCK))
    p_s = masked_softmax(jnp.einsum('bqkgd,bkqsd->bkgqs', qr, k_s) * scale, m_s[:, :, None])
    o_s = jnp.einsum('bkgqs,bkqsd->bqkgd', p_s.astype(v_s.dtype), v_s)
    dpos = pos_q[:, None] - pos_w[None, :]
    m_w = (dpos >= 0) & (dpos < WINDOW) & (pos_w[None, :] >= 0)
    p_w = masked_softmax(jnp.einsum('bqkgd,bskd->bkgqs', qr, k_w) * scale, m_w)
    o_w = jnp.einsum('bkgqs,bskd->bqkgd', p_w.astype(v_w.dtype), v_w)
    g = jax.nn.sigmoid(gate).reshape(B, Tq, N_KV, GROUP, 3)
    o = g[..., 0:1] * o_c + g[..., 1:2] * o_s + g[..., 2:3] * o_w
    return o.reshape(B, Tq, B_WIDTH)


def nsa_prompt(q_n, q_r, gate, kc, vc, k_sel, v_sel, k_win, v_win):
    B, T = q_n.shape[:2]
    kw_pad = jnp.pad(k_win, ((0, 0), (WINDOW, 0), (0, 0), (0, 0)))
    vw_pad = jnp.pad(v_win, ((0, 0), (WINDOW, 0), (0, 0), (0, 0)))
    bi = jnp.arange(B)[:, None, None, None]
    ki = jnp.arange(N_KV)[None, :, None, None]

    def fetch(pos):
        p = jnp.clip(pos, 0, T - 1)
        return k_sel[bi, p, ki], v_sel[bi, p, ki]

    def one_block(s):
        pos_q = s + jnp.arange(Q_BLOCK, dtype=jnp.int32)
        sl = lambda a, n: lax.dynamic_slice_in_dim(a, s, n, axis=1)
        pos_w = s - WINDOW + jnp.arange(Q_BLOCK + WINDOW, dtype=jnp.int32)
        return nsa_attend(sl(q_n, Q_BLOCK), sl(q_r, Q_BLOCK), sl(gate, Q_BLOCK), pos_q, kc, vc, fetch,
                          sl(kw_pad, Q_BLOCK + WINDOW), sl(vw_pad, Q_BLOCK + WINDOW), pos_w, T)

    starts = jnp.arange(T // Q_BLOCK, dtype=jnp.int32) * Q_BLOCK
    o = lax.map(one_block, starts)
    return o.transpose(1, 0, 2, 3).reshape(B, T, B_WIDTH)


def nsa_sample(q_n, q_r, gate, kc, vc, ks_new, vs_new, k_w, v_w, pos_w, sk_pool, sv_pool, page_table):
    Bd = q_n.shape[0]
    bi = jnp.arange(Bd)[:, None, None, None]
    ki = jnp.arange(N_KV)[None, :, None, None]

    def fetch(pos):
        past = (pos < PAST_LEN)[..., None]
        pc = jnp.clip(pos, 0, PAST_LEN - 1)
        phys = page_table[bi, pc // PAGE_SIZE]
        off = pc % PAGE_SIZE
        pn = jnp.clip(pos - PAST_LEN, 0, DEC_SEQ - 1)
        k = jnp.where(past, sk_pool[phys, off, ki], ks_new[bi, pn, ki])
        v = jnp.where(past, sv_pool[phys, off, ki], vs_new[bi, pn, ki])
        return k, v

    pos_q = PAST_LEN + jnp.arange(DEC_SEQ, dtype=jnp.int32)
    return nsa_attend(q_n, q_r, gate, pos_q, kc, vc, fetch, k_w, v_w, pos_w, PAST_LEN + DEC_SEQ)


def gather_pages(pool, page_table):
    Bd, n_pages = page_table.shape
    return pool[page_table].reshape(Bd, n_pages * PAGE_SIZE, N_KV, HEAD_DIM)


def merge_and_ffn(x, o_a, o_b, merge_gate, w_branch, w_out, g_norm2, w_up, w_down):
    y_a = o_a @ w_branch[:A_WIDTH]
    y_b = o_b @ w_branch[A_WIDTH:]
    g_a, g_b = jnp.split(merge_gate, 2, axis=-1)
    x = x + (jax.nn.sigmoid(g_a) * y_a + jax.nn.sigmoid(g_b) * y_b) @ w_out
    h = rms_norm(x, g_norm2)
    return x + jnp.square(jax.nn.relu(h @ w_up)) @ w_down


def setup_inputs(seed: int = 0) -> dict:
    key = jax.random.key(seed)
    k = jax.random.split(key, 40)
    nrm = lambda kk, shape, scale: jax.random.normal(kk, shape, jnp.float32) * scale
    gain = lambda kk, n: 1.0 + nrm(kk, (DEPTH, n), 0.05)
    n_pages = PAST_LEN // PAGE_SIZE
    n_phys = (5 * DEC_BATCH * n_pages) // 4
    w_buf = min(WINDOW, PAST_LEN)
    pool = (DEPTH, n_phys, PAGE_SIZE, N_KV, HEAD_DIM)
    perm = jax.random.permutation(k[9], n_phys)
    page_table = perm[:DEC_BATCH * n_pages].reshape(DEC_BATCH, n_pages).astype(jnp.int32)
    return {
        'x_prompt': nrm(k[0], (BATCH, SEQ, D_MODEL), 1.0),
        'x_sample': nrm(k[1], (DEC_BATCH, DEC_SEQ, D_MODEL), 1.0),
        'cache_k_cmp': nrm(k[2], pool, 1.0),
        'cache_v_cmp': nrm(k[3], pool, 1.0),
        'cache_k_sel': nrm(k[4], pool, 1.0),
        'cache_v_sel': nrm(k[5], pool, 1.0),
        'cache_k_win': nrm(k[6], (DEPTH, DEC_BATCH, w_buf, N_KV, HEAD_DIM), 1.0),
        'cache_v_win': nrm(k[7], (DEPTH, DEC_BATCH, w_buf, N_KV, HEAD_DIM), 1.0),
        'page_table': page_table,
        'g_norm1': gain(k[10], D_MODEL),
        'w_in': nrm(k[11], (DEPTH, D_MODEL, IN_COLS), D_MODEL ** -0.5),
        'ln_v_g': gain(k[12], A_WIDTH),
        'ln_v_b': nrm(k[13], (DEPTH, A_WIDTH), 0.02),
        'w_s': nrm(k[14], (DEPTH, A_GROUPS, CHUNK, CHUNK), 0.5 * CHUNK ** -0.5),
        'b_s': 1.0 + nrm(k[15], (DEPTH, A_GROUPS, CHUNK), 0.1),
        'g_q': gain(k[16], HEAD_DIM),
        'g_kc': gain(k[17], HEAD_DIM),
        'g_ks': gain(k[18], HEAD_DIM),
        'g_kw': gain(k[19], HEAD_DIM),
        'w_ck1': nrm(k[20], (DEPTH, CMP_BLOCK * HEAD_DIM, HEAD_DIM), (CMP_BLOCK * HEAD_DIM) ** -0.5),
        'w_ck2': nrm(k[21], (DEPTH, HEAD_DIM, HEAD_DIM), HEAD_DIM ** -0.5),
        'pe_k': nrm(k[22], (DEPTH, CMP_BLOCK, HEAD_DIM), 0.1),
        'w_cv1': nrm(k[23], (DEPTH, CMP_BLOCK * HEAD_DIM, HEAD_DIM), (CMP_BLOCK * HEAD_DIM) ** -0.5),
        'w_cv2': nrm(k[24], (DEPTH, HEAD_DIM, HEAD_DIM), HEAD_DIM ** -0.5),
        'pe_v': nrm(k[25], (DEPTH, CMP_BLOCK, HEAD_DIM), 0.1),
        'w_branch': nrm(k[26], (DEPTH, A_WIDTH + B_WIDTH, D_MODEL), (A_WIDTH + B_WIDTH) ** -0.5),
        'w_out': nrm(k[27], (DEPTH, D_MODEL, D_MODEL), D_MODEL ** -0.5),
        'g_norm2': gain(k[28], D_MODEL),
        'w_up': nrm(k[29], (DEPTH, D_MODEL, D_FF), D_MODEL ** -0.5),
        'w_down': nrm(k[30], (DEPTH, D_FF, D_MODEL), D_FF ** -0.5),
    }


def reference(x_prompt, x_sample, cache_k_cmp, cache_v_cmp, cache_k_sel, cache_v_sel, cache_k_win,
              cache_v_win, page_table, g_norm1, w_in, ln_v_g, ln_v_b, w_s, b_s, g_q, g_kc, g_ks, g_kw,
              w_ck1, w_ck2, pe_k, w_cv1, w_cv2, pe_v, w_branch, w_out, g_norm2, w_up, w_down):
    pos_p = jnp.arange(SEQ, dtype=jnp.int32)
    pos_d = PAST_LEN + jnp.arange(DEC_SEQ, dtype=jnp.int32)
    w_p = min(WINDOW, SEQ)
    chunk_start = ((SEQ - 1) // CHUNK) * CHUNK
    xp, xs = x_prompt, x_sample
    new_p = [[] for _ in range(7)]
    new_s = [[] for _ in range(7)]
    for l in range(DEPTH):
        proj = lambda x, pos: in_projection(x, pos, g_norm1[l], w_in[l], ln_v_g[l], ln_v_b[l], g_q[l], g_ks[l], g_kw[l])
        u, v_n, q_n, q_r, kcr, vcr, ks, vs, kw, vw, gate, mg = proj(xp, pos_p)
        o_a = gmlp_mix(u, v_n, w_s[l], b_s[l])
        kc = rms_norm(compress(kcr, w_ck1[l], w_ck2[l], pe_k[l]), g_kc[l])
        vc = compress(vcr, w_cv1[l], w_cv2[l], pe_v[l])
        o_b = nsa_prompt(q_n, q_r, gate, kc, vc, ks, vs, kw, vw)
        xp = merge_and_ffn(xp, o_a, o_b, mg, w_branch[l], w_out[l], g_norm2[l], w_up[l], w_down[l])
        for lst, a in zip(new_p, (kcr, vcr, ks, vs, kw[:, SEQ - w_p:], vw[:, SEQ - w_p:], v_n[:, chunk_start:])):
            lst.append(a)
        u, v_n, q_n, q_r, kcr, vcr, ks, vs, kw, vw, gate, mg = proj(xs, pos_d)
        o_a = gmlp_mix(u, v_n, w_s[l], b_s[l])
        kc_all = jnp.concatenate([gather_pages(cache_k_cmp[l], page_table), kcr], axis=1)
        vc_all = jnp.concatenate([gather_pages(cache_v_cmp[l], page_table), vcr], axis=1)
        kc = rms_norm(compress(kc_all, w_ck1[l], w_ck2[l], pe_k[l]), g_kc[l])
        vc = compress(vc_all, w_cv1[l], w_cv2[l], pe_v[l])
        w_buf = cache_k_win.shape[2]
        k_w = jnp.concatenate([cache_k_win[l], kw], axis=1)
        v_w = jnp.concatenate([cache_v_win[l], vw], axis=1)
        pos_w = PAST_LEN - w_buf + jnp.arange(w_buf + DEC_SEQ, dtype=jnp.int32)
        o_b = nsa_sample(q_n, q_r, gate, kc, vc, ks, vs, k_w, v_w, pos_w, cache_k_sel[l], cache_v_sel[l], page_table)
        xs = merge_and_ffn(xs, o_a, o_b, mg, w_branch[l], w_out[l], g_norm2[l], w_up[l], w_down[l])
        for lst, a in zip(new_s, (kcr, vcr, ks, vs, kw, vw, v_n)):
            lst.append(a)
    p_kc, p_vc, p_ks, p_vs, p_kw, p_vw, p_va = [jnp.stack(a) for a in new_p]
    s_kc, s_vc, s_ks, s_vs, s_kw, s_vw, s_va = [jnp.stack(a) for a in new_s]
    return (xp, xs, p_kc, p_vc, p_ks, p_vs, p_kw, p_vw, p_va, s_kc, s_vc, s_ks, s_vs, s_kw, s_vw, s_va)
```

```python
import contextlib
import numpy as np
import concourse.bass as bass
import concourse.mybir as mybir
from concourse.bass_utils import run_bass_kernel_spmd

F32 = mybir.dt.float32
BF16 = mybir.dt.bfloat16
I32 = mybir.dt.int32
AF = mybir.ActivationFunctionType
ALU = mybir.AluOpType
AX = mybir.AxisListType

D = 1024
SEQ = 8192
NT = 64
NOWN = 16
PAST = 16384
NEGM = -30000.0
DBG_NT = 64
DBG_NOWN = 16
DBG_SKIP = 0
DBG_SAMPLE = 1
DBG_NSEQ = 4
DBG_BAR = 0
DBG_DELAY = 0
DBG_J = 8
DBG_RB0 = 0
DBG_RB1 = 16
DBG_NPAGES = 5120
DBG_OUT = 0
DBG_MAXOPS = 10**9
DBG_POOLN = 1
IN_COLS = 4376
C_U, C_V, C_Q, C_KV, C_G, C_MG = 0, 512, 1024, 1536, 2304, 2328


class _Op:
    __slots__ = ("eng", "fn", "deps", "dma", "needed", "value", "sem", "phase")


class Prog:
    ENGS = ("pe", "act", "dve", "pool", "sp")
    NDMA = 8

    def __init__(self, nc):
        self.nc = nc
        self.st = contextlib.ExitStack()
        self.esem = {e: self.st.enter_context(nc.semaphore("s_" + e)) for e in self.ENGS}
        self.dsem = {}
        for e in ("sp", "act", "pool"):
            for s in range(self.NDMA):
                self.dsem[(e, s)] = self.st.enter_context(nc.semaphore("d_%s%d" % (e, s)))
        self.ops = {e: [] for e in self.ENGS}
        self.last_w = {}
        self.readers = {}
        self.dma_rr = {e: 0 for e in self.ENGS}
        self.dma_last = {}
        self.dma_cnt = {k: 0 for k in self.dsem}
        self.out_dmas = []
        self.pending_dmas = []
        self.cnt = {e: 0 for e in self.ENGS}
        self.seen = {e: {} for e in self.ENGS}
        self.phase = 0
        self.nops = 0

    def op(self, eng, fn, r=(), w=(), dma=False, out=False):
        if fn is not None and self.nops >= DBG_MAXOPS:
            return None
        o = _Op()
        o.eng = eng; o.fn = fn; o.dma = dma; o.needed = False; o.value = None; o.sem = None
        o.phase = self.phase
        self.nops += 1
        deps = {}
        for k in r:
            lw = self.last_w.get(k)
            if lw is not None:
                deps[id(lw)] = lw
        for k in w:
            lw = self.last_w.get(k)
            if lw is not None:
                deps[id(lw)] = lw
            for rd in self.readers.get(k, ()):
                deps[id(rd)] = rd
        if dma:
            slot = self.dma_rr[eng] % (DBG_POOLN if eng == 'pool' else self.NDMA)
            self.dma_rr[eng] += 1
            key = (eng, slot)
            prev = self.dma_last.get(key)
            if prev is not None:
                deps[id(prev)] = prev
            self.dma_last[key] = o
            self.dma_cnt[key] += 1
            o.sem = key
            o.value = 16 * self.dma_cnt[key]
            o.needed = True
            self.pending_dmas.append(o)
            if out:
                self.out_dmas.append(o)
        dl = []
        for d in deps.values():
            if d is o or d.phase != self.phase:
                continue
            if (not d.dma) and d.eng == "pe" and eng == "pe" and not dma:
                continue
            d.needed = True
            dl.append(d)
        o.deps = dl
        for k in w:
            self.last_w[k] = o
            self.readers[k] = []
        for k in r:
            if k in w:
                continue
            lst = self.readers.setdefault(k, [])
            if not dma:
                lst[:] = [x for x in lst if x.dma or x.eng != eng]
            lst.append(o)
        self.ops[eng].append(o)
        return o

    def flush(self):
        nc = self.nc
        marks = []
        for e in self.ENGS:
            for o in reversed(self.ops[e]):
                if (not o.dma) and o.fn is not None:
                    o.needed = True
                    marks.append(o)
                    break
        bdeps = marks + list(self.pending_dmas)
        for e in self.ENGS:
            b = self.op(e, None)
            b.deps = list(bdeps)
        self.pending_dmas = []
        for e in self.ENGS:
            for o in self.ops[e]:
                if o.dma or o.fn is None:
                    continue
                if o.needed:
                    self.cnt[e] += 1
                    o.value = self.cnt[e]
        with nc.Block() as block:
            def run(ename, eng):
                seen = self.seen[ename]
                for o in self.ops[ename]:
                    waits = {}
                    for d in o.deps:
                        s = ("d", d.sem) if d.dma else ("e", d.eng)
                        if waits.get(s, 0) < d.value:
                            waits[s] = d.value
                    for s, v in waits.items():
                        if seen.get(s, 0) >= v:
                            continue
                        seen[s] = v
                        sem = self.dsem[s[1]] if s[0] == "d" else self.esem[s[1]]
                        eng.wait_ge(sem, v)
                    if o.fn is None:
                        continue
                    ins = o.fn(eng)
                    if o.dma:
                        ins.then_inc(self.dsem[o.sem], 16)
                    elif o.needed:
                        ins.then_inc(self.esem[ename], 1)

            @block.tensor
            def _(eng):
                run("pe", eng)

            @block.scalar
            def _(eng):
                run("act", eng)

            @block.vector
            def _(eng):
                run("dve", eng)

            @block.gpsimd
            def _(eng):
                run("pool", eng)

            @block.sync
            def _(eng):
                run("sp", eng)
        self.ops = {e: [] for e in self.ENGS}
        self.last_w = {}
        self.readers = {}
        self.dma_last = {}
        self.phase += 1

    def finish(self):
        self.flush()
        self.st.close()


class Ring:
    def __init__(self, st, alloc, name, shape, dt, n):
        self.t = [st.enter_context(alloc(name + str(i), shape, dt)) for i in range(n)]
        self.name = name
        self.n = n
        self.i = 0

    def next(self):
        k = self.i % self.n
        self.i += 1
        return self.t[k], self.name + str(k)


def build(stage=99):
    nc = bass.Bass("TRN2", target_bir_lowering=False)
    di = lambda n, s, d=F32: nc.dram_tensor(n, list(s), d, kind="ExternalInput").ap()
    do = lambda n, s, d=F32: nc.dram_tensor(n, list(s), d, kind="ExternalOutput").ap()
    xb = di("xb", [SEQ, D]); xo = di("xo", [NOWN * 128, D]); xs = di("xs", [16, D])
    g1 = di("g1", [1, D]); w_in = di("w_in", [D, IN_COLS])
    g_ks = di("g_ks", [1, 64]); g_kw = di("g_kw", [1, 64]); g_q = di("g_q", [1, 64]); g_kc = di("g_kc", [1, 64])
    w_ck1 = di("w_ck1", [2048, 64]); w_ck2 = di("w_ck2", [64, 64]); pe_k = di("pe_k", [32, 64])
    w_cv1 = di("w_cv1", [2048, 64]); w_cv2 = di("w_cv2", [64, 64]); pe_v = di("pe_v", [32, 64])
    ident = di("ident", [128, 128])
    cosA = di("cosA", [128, NT, 8]); sinA = di("sinA", [128, NT, 8])
    eind = di("eind", [64, SEQ])
    lng = di("lng", [1, 512]); lnb = di("lnb", [1, 512]); w_s = di("w_s", [8, 128, 128]); b_s = di("b_s", [8, 128])
    w_br = di("w_br", [D, D]); w_out = di("w_out", [D, D]); g2 = di("g2", [1, D])
    w_up = di("w_up", [D, 4096]); w_dn = di("w_dn", [4096, D])
    cosO = di("cosO", [128, NOWN + 1, 8]); sinO = di("sinO", [128, NOWN + 1, 8])
    msel = di("msel", [128, 4, 128]); mwin = di("mwin", [128, 8, 128])
    mcq = di("mcq", [128, 32]); mct = di("mct", [32, 128]); zw = di("zw", [32, 288])
    m1c = di("m1c", [128, 256]); m2c = di("m2c", [128, 256]); tril = di("tril", [128, 128])
    o_pkv = do("o_pkv", [SEQ, 512])
    o_pw = do("o_pw", [512, 256])
    o_pva = do("o_pva", [128, 512])
    o_yp = do("o_yp", [NOWN * 128, D])
    if DBG_SAMPLE:
        pkc = di("pkc", [DBG_NPAGES, 16384]); pvc = di("pvc", [DBG_NPAGES, 16384]); pks = di("pks", [DBG_NPAGES, 16384]); pvs = di("pvs", [DBG_NPAGES, 16384])
        pt = di("pt", [128, 4], I32); kwin = di("kwin", [4, 512, 128]); vwin = di("vwin", [4, 512, 128])
        mnew = di("mnew", [128, 4, 16]); mw4 = di("mw4", [128, 4, 16]); cmask = di("cmask", [128, 8]); oh = di("oh", [24, 24, 128])
    o_skv = do("o_skv", [16, 768]); o_sva = do("o_sva", [16, 512]); o_ys = do("o_ys", [16, D])

    P = Prog(nc)
    dbg_n = [0]

    def dbg_out(st, src, shape, keys):
        if not DBG_OUT:
            return
        name = "o_dbg%d" % dbg_n[0]
        dbg_n[0] += 1
        d = do(name, shape)
        t = st.enter_context(nc.sbuf_tensor(name + "_s", list(shape), F32))
        P.op("act", lambda e: e.copy(out=t[:], in_=src), r=keys, w=[name])
        P.op("sp", lambda e: e.dma_start(out=d, in_=t[:]), r=[name], dma=True, out=True)
    sbuf = lambda n, s, d: nc.sbuf_tensor(n, list(s), d)
    psum = lambda n, s, d: nc.psum_tensor(n, list(s), d)

    def dma(eng, out, in_, r=(), w=(), o=False, **kw):
        P.op(eng, lambda e: e.dma_start(out=out, in_=in_, **kw), r=r, w=w, dma=True, out=o)

    def mm(out, lhsT, rhs, start, stop, r, w):
        P.op("pe", lambda e: e.matmul(out, lhsT=lhsT, rhs=rhs, start=start, stop=stop), r=r, w=w)

    def tr(out, in_, idn, r, w):
        P.op("pe", lambda e: e.transpose(out=out, in_=in_, identity=idn), r=r, w=w)

    def act(out, in_, func, r, w, **kw):
        P.op("act", lambda e: e.activation(out=out, in_=in_, func=func, **kw), r=r, w=w)

    def acopy(out, in_, r, w):
        P.op("act", lambda e: e.copy(out=out, in_=in_), r=r, w=w)

    def vcopy(out, in_, r, w):
        P.op("dve", lambda e: e.tensor_copy(out=out, in_=in_), r=r, w=w)

    def tt(out, in0, in1, op, r, w, eng="dve"):
        P.op(eng, lambda e: e.tensor_tensor(out=out, in0=in0, in1=in1, op=op), r=r, w=w)

    def ts(out, in0, s1, s2, op0, op1, r, w, eng="dve"):
        if s2 is None:
            P.op(eng, lambda e: e.tensor_scalar(out=out, in0=in0, scalar1=s1, scalar2=None, op0=op0), r=r, w=w)
        else:
            P.op(eng, lambda e: e.tensor_scalar(out=out, in0=in0, scalar1=s1, scalar2=s2, op0=op0, op1=op1), r=r, w=w)

    def stt(out, in0, scalar, in1, op0, op1, r, w, eng="dve"):
        P.op(eng, lambda e: e.scalar_tensor_tensor(out=out, in0=in0, scalar=scalar, in1=in1, op0=op0, op1=op1), r=r, w=w)

    def recip(out, in_, r, w):
        P.op("dve", lambda e: e.reciprocal(out=out, in_=in_), r=r, w=w)

    def memset(eng, ap, val, w):
        P.op(eng, lambda e: e.memset(ap, val), w=w)

    with contextlib.ExitStack() as S0:
        sb0 = lambda n, s, d: S0.enter_context(sbuf(n, s, d))
        identb = sb0("identb", [128, 128], BF16)
        epsc = sb0("epsc", [128, 1], F32)
        g1b = sb0("g1b", [128, D], F32)
        identf = sb0("identf", [128, 128], F32)
        dma("sp", identf[:], ident, w=["identf"])
        acopy(identb[:], identf[:], r=["identf"], w=["identb"])
        memset("dve", epsc[:], 1e-6, w=["eps"])
        dma("sp", g1b[:], g1.partition_broadcast(128)[:, 0, :], w=["g1b"])

        RX = Ring(S0, sbuf, "xt", [128, D], F32, 2)
        RJ = Ring(S0, sbuf, "junk", [128, D], BF16, 1)
        RSS = Ring(S0, sbuf, "ss", [128, 4], F32, 2)
        RH = Ring(S0, sbuf, "hb", [128, D], BF16, 2)

        def norm_T(xt, kx, gbc, kg, n, dst, kdst, PTR):
            junk, kj = RJ.next()
            ss, kss = RSS.next()
            act(junk[:n], xt[:n], AF.Square, r=[kx], w=[kj, kss], accum_out=ss[:n, 0:1])
            act(ss[:n, 1:2], ss[:n, 0:1], AF.Sqrt, r=[kss, "eps"], w=[kss], bias=epsc[:n, 0:1], scale=1.0 / D)
            recip(ss[:n, 2:3], ss[:n, 1:2], r=[kss], w=[kss])
            hb, khb = RH.next()
            stt(hb[:n], xt[:n], ss[:n, 2:3], gbc[:n], ALU.mult, ALU.mult, r=[kx, kss, kg], w=[khb])
            ptr, kptr = PTR.next()
            for k in range(8):
                tr(ptr[:, k, :n], hb[:n, k * 128:(k + 1) * 128], identb[:n, :n], r=[khb, "identb"], w=[kptr])
            acopy(dst, ptr[:, :, :n], r=[kptr], w=[kdst])

        WST = Ring(S0, sbuf, "wst", [128, 2048], F32, 2)
        NTOK = NOWN * 128
        KSNT = sb0("KSNT", [128, 128], BF16); KWNT = sb0("KWNT", [128, 128], BF16)
        VSN = sb0("VSN", [128, 2, 64], BF16); VWN = sb0("VWN", [128, 2, 64], BF16)
        for t_, k_ in ((KSNT, "KSNT"), (KWNT, "KWNT")):
            memset("dve", t_[:], 0.0, w=[k_])
        for t_, k_ in ((VSN, "VSN"), (VWN, "VWN")):
            memset("dve", t_[:], 0.0, w=[k_])
        OATs = sb0("OATs", [128, 4, 16], BF16); OBTs = sb0("OBTs", [128, 4, 16], BF16)
        H2Ts = sb0("H2Ts", [128, 8, 16], BF16)
        QNTs = sb0("QNTs", [128, 4, 16], BF16); QRTs = sb0("QRTs", [128, 4, 16], BF16)
        sgTs = sb0("sgTs", [24, 16], BF16)
        memset("dve", OBTs[:], 0.0, w=["OBTs"])
        BIG = sb0("BIG", [128, 2, SEQ], BF16)
        KVCT = BIG
        OAT = BIG[:, 0, :].rearrange("p (c t) -> p c t", c=4)
        OBT = BIG[:, 1, :].rearrange("p (c t) -> p c t", c=4)

        def load_cast(dst, src, key, n0=0, n1=128):
            C = dst.shape[-1]
            for c0 in range(0, C, 2048):
                c1 = min(C, c0 + 2048)
                stg, kstg = WST.next()
                dma("sp", stg[n0:n1, 0:c1 - c0], src[:, c0:c1], w=[kstg])
                acopy(dst[:, c0:c1], stg[n0:n1, 0:c1 - c0], r=[kstg], w=[key])

        with contextlib.ExitStack() as SA:
            sba = lambda n, s, d: SA.enter_context(sbuf(n, s, d))
            KE0 = sba("KE0", [128, SEQ], BF16)
            KE1 = sba("KE1", [128, SEQ], BF16)
            KWT = sba("KWT", [128, SEQ], BF16)
            VS = sba("VS", [128, NT, 2, 65], BF16)
            VW = sba("VW", [128, NT, 2, 65], BF16)
            KCT = sba("KCT", [128, 512], BF16)
            VC = sba("VC", [128, 4, 2, 65], BF16)
            w_in_v = w_in.rearrange("(k p) c -> p k c", p=128)
            load_cast(KE0[64:128, :], eind, "KE0", 64, 128)
            load_cast(KE1[0:64, :], eind, "KE1", 0, 64)
            memset("dve", VS[:, :, :, 64:65], 1.0, w=["VS"])
            memset("dve", VW[:, :, :, 64:65], 1.0, w=["VW"])

            with contextlib.ExitStack() as SA1:
                sb1 = lambda n, s, d: SA1.enter_context(sbuf(n, s, d))
                if DBG_NT < NT:
                    memset("dve", KVCT[:], 0.0, w=["KVCT"])
                with contextlib.ExitStack() as SP:
                    sbp = lambda n, s, d: SP.enter_context(sbuf(n, s, d))
                    cosA_s = sbp("cosA_s", [128, NT, 8], F32)
                    sinA_s = sbp("sinA_s", [128, NT, 8], F32)
                    gkb = sbp("gkb", [128, 4, 64], F32)
                    wkv = sbp("wkv", [128, 8, 768], BF16)
                    dma("sp", cosA_s[:], cosA, w=["cosA"])
                    dma("sp", sinA_s[:], sinA, w=["cosA2"])
                    for wi, gsrc in enumerate((g_ks, g_ks, g_kw, g_kw)):
                        dma("sp", gkb[:, wi, :], gsrc.partition_broadcast(128)[:, 0, :], w=["gkb"])
                    for k in range(8):
                        load_cast(wkv[:, k, :], w_in_v[:, k, C_KV:C_KV + 768], "wkv")
                    RHT = Ring(SP, sbuf, "hT", [128, 8, 128], BF16, 2)
                    RST = Ring(SP, sbuf, "stg", [128, 768], F32, 2)
                    RT1 = Ring(SP, sbuf, "kt1", [128, 256], F32, 2)
                    RT2 = Ring(SP, sbuf, "kt2", [128, 256], F32, 2)
                    RT3 = Ring(SP, sbuf, "kt3", [128, 4, 2, 8], F32, 2)
                    RT4 = Ring(SP, sbuf, "kt4", [128, 4, 2, 8], F32, 2)
                    RKS = Ring(SP, sbuf, "kss", [128, 12], F32, 2)
                    RKB = Ring(SP, sbuf, "kb", [128, 512], BF16, 2)
                    PTR = Ring(SP, psum, "ptr", [128, 8, 128], BF16, 2)
                    PKV = Ring(SP, psum, "pkv", [128, 1024], F32, 2)
                    PTK = Ring(SP, psum, "ptk", [128, 4, 128], BF16, 2)

                    def kside_post(pk, kpk, n, cos, sin, kcs, st_, kst):
                        acopy(st_[:n, :], pk[:n, 0:768], r=[kpk], w=[kst])
                        kk = st_[:n, 256:768].rearrange("p (w c) -> p w c", c=256)[:, :, 0:128] \
                            .rearrange("p w (k d) -> p w k d", d=64)
                        t1, k1 = RT1.next()
                        t2, k2 = RT2.next()
                        t3, k3 = RT3.next()
                        t4, k4 = RT4.next()
                        ks_, kks = RKS.next()
                        v4 = lambda t_: t_[:n, :].rearrange("p (w k d) -> p w k d", w=2, k=2)
                        tt(v4(t1), kk, kk, ALU.mult, r=[kst], w=[k1])
                        P.op("dve", lambda e: e.tensor_reduce(out=ks_[:n, 0:4], in_=t1[:n, :].rearrange("p (a d) -> p a d", d=64),
                                                             axis=AX.X, op=ALU.add), r=[k1], w=[kks])
                        act(ks_[:n, 4:8], ks_[:n, 0:4], AF.Sqrt, r=[kks, "eps"], w=[kks], bias=epsc[:n, 0:1], scale=1.0 / 64)
                        recip(ks_[:n, 8:12], ks_[:n, 4:8], r=[kks], w=[kks])
                        rk = ks_[:n, 8:12].rearrange("p (w k) -> p w k", w=2).unsqueeze(3).to_broadcast([n, 2, 2, 64])
                        tt(v4(t2), kk, rk, ALU.mult, r=[kst, kks], w=[k2])
                        tt(t1[:n, :], t2[:n, :], gkb[:n].rearrange("p a d -> p (a d)"), ALU.mult, r=[k2, "gkb"], w=[k1])
                        X = t1[:n, :].rearrange("p (a d) -> p a d", d=64)[:, :, 0:16].rearrange("p a (h e) -> p a h e", e=8)
                        cb = cos.unsqueeze(1).unsqueeze(1).to_broadcast([n, 4, 2, 8])
                        sb_ = sin.unsqueeze(1).to_broadcast([n, 4, 8])
                        tt(t3[:n], X, cb, ALU.mult, r=[k1, kcs, kcs + "2"], w=[k3])
                        tt(t4[:n, :, 0, :], X[:, :, 1, :], sb_, ALU.mult, r=[k1, kcs, kcs + "2"], w=[k4])
                        tt(t4[:n, :, 1, :], X[:, :, 0, :], sb_, ALU.mult, r=[k1, kcs, kcs + "2"], w=[k4])
                        vcopy(kk, v4(t1), r=[k1], w=[kst])
                        kk8 = kk[:, :, :, 0:16].rearrange("p w k (h e) -> p w k h e", e=8)
                        t3v = t3[:n].rearrange("p (w k) h e -> p w k h e", w=2)
                        t4v = t4[:n].rearrange("p (w k) h e -> p w k h e", w=2)
                        for w_ in range(2):
                            tt(kk8[:, w_, :, 0, :], t3v[:, w_, :, 0, :], t4v[:, w_, :, 0, :], ALU.subtract, r=[k3, k4], w=[kst])
                            tt(kk8[:, w_, :, 1, :], t3v[:, w_, :, 1, :], t4v[:, w_, :, 1, :], ALU.add, r=[k3, k4], w=[kst])

                    for t in range(DBG_NT):
                        xt, kx = RX.next()
                        dma("sp", xt[:], xb[128 * t:128 * (t + 1), :], w=[kx])
                        hT, khT = RHT.next()
                        norm_T(xt, kx, g1b, "g1b", 128, hT[:], khT, PTR)
                        pk, kpk = PKV.next()
                        for k in range(8):
                            mm(pk[:, 0:512], hT[:, k, :], wkv[:, k, 0:512], k == 0, k == 7, r=[khT, "wkv"], w=[kpk])
                        for k in range(8):
                            mm(pk[:, 512:768], hT[:, k, :], wkv[:, k, 512:768], k == 0, k == 7, r=[khT, "wkv"], w=[kpk])
                        st_, kst = RST.next()
                        kside_post(pk, kpk, 128, cosA_s[:, t, :], sinA_s[:, t, :], "cosA", st_, kst)
                        dma("sp", o_pkv[128 * t:128 * (t + 1), :], st_[:, 0:512], r=[kst], o=True)
                        if t >= 60:
                            dma("sp", o_pw[128 * (t - 60):128 * (t - 59), :], st_[:, 512:768], r=[kst], o=True)
                        kb, kkb = RKB.next()
                        acopy(kb[:, 0:256], pk[:, 0:256], r=[kpk], w=[kkb])
                        vcopy(kb[:, 256:512].rearrange("p (w c) -> p w c", c=128),
                              st_[:, 256:768].rearrange("p (w c) -> p w c", c=256)[:, :, 0:128], r=[kst], w=[kkb])
                        acopy(VS[:, t, :, 0:64], pk[:, 384:512].rearrange("p (k d) -> p k d", d=64), r=[kpk], w=["VS"])
                        acopy(VW[:, t, :, 0:64], pk[:, 640:768].rearrange("p (k d) -> p k d", d=64), r=[kpk], w=["VW"])
                        ptk, kptk = PTK.next()
                        for q in range(4):
                            tr(ptk[:, q, :], kb[:, q * 128:(q + 1) * 128], identb[:], r=[kkb, "identb"], w=[kptk])
                        sl = slice(128 * t, 128 * (t + 1))
                        acopy(KVCT[:, :, sl], ptk[:, 0:2, :], r=[kptk], w=["KVCT"])
                        acopy(KE0[0:64, sl], ptk[0:64, 2, :], r=[kptk], w=["KE0"])
                        acopy(KE1[64:128, sl], ptk[64:128, 2, :], r=[kptk], w=["KE1"])
                        acopy(KWT[:, sl], ptk[:, 3, :], r=[kptk], w=["KWT"])
                    cosS_s = sbp("cosS_s", [16, 8], F32)
                    sinS_s = sbp("sinS_s", [16, 8], F32)
                    dma("sp", cosS_s[:], cosO[0:16, NOWN, :], w=["cosS"])
                    dma("sp", sinS_s[:], sinO[0:16, NOWN, :], w=["cosS2"])
                    xt, kx = RX.next()
                    dma("sp", xt[:16], xs, w=[kx])
                    hT, khT = RHT.next()
                    norm_T(xt, kx, g1b, "g1b", 16, hT[:, :, :16], khT, PTR)
                    pk, kpk = PKV.next()
                    for k in range(8):
                        mm(pk[:16, 0:512], hT[:, k, :16], wkv[:, k, 0:512], k == 0, k == 7, r=[khT, "wkv"], w=[kpk])
                    for k in range(8):
                        mm(pk[:16, 512:768], hT[:, k, :16], wkv[:, k, 512:768], k == 0, k == 7, r=[khT, "wkv"], w=[kpk])
                    st_, kst = RST.next()
                    kside_post(pk, kpk, 16, cosS_s[:, :], sinS_s[:, :], "cosS", st_, kst)
                    dma("sp", o_skv, st_[:16, :], r=[kst], o=True)
                    kb, kkb = RKB.next()
                    acopy(kb[:16, 0:256], pk[:16, 0:256], r=[kpk], w=[kkb])
                    vcopy(kb[:16, 256:512].rearrange("p (w c) -> p w c", c=128),
                          st_[:16, 256:768].rearrange("p (w c) -> p w c", c=256)[:, :, 0:128], r=[kst], w=[kkb])
                    acopy(VSN[0:16, :, :], pk[:16, 384:512].rearrange("p (k d) -> p k d", d=64), r=[kpk], w=["VSN"])
                    acopy(VWN[0:16, :, :], pk[:16, 640:768].rearrange("p (k d) -> p k d", d=64), r=[kpk], w=["VWN"])
                    ptk, kptk = PTK.next()
                    for q in range(2, 4):
                        tr(ptk[:, q, :16], kb[:16, q * 128:(q + 1) * 128], identb[:16, :16], r=[kkb, "identb"], w=[kptk])
                    acopy(KSNT[:, 0:16], ptk[:, 2, :16], r=[kptk], w=["KSNT"])
                    acopy(KWNT[:, 0:16], ptk[:, 3, :16], r=[kptk], w=["KWNT"])
                    P.flush()
                BD1 = [sb1("BD1k", [128, 32, 128], BF16), sb1("BD1v", [128, 32, 128], BF16)]
                BD2 = [sb1("BD2k", [128, 128], BF16), sb1("BD2v", [128, 128], BF16)]
                BDo = sb1("BDo", [128, 128], BF16)
                pe2 = sb1("pe2", [32, 2, 128], BF16)
                peT = sb1("peT", [128, 2, 32], BF16)
                gkc = sb1("gkc", [128, 1], F32)
                biasc = sb1("biasc", [128, 2], F32)
                hc = [sb1("hck", [128, 512], BF16), sb1("hcv", [128, 512], BF16)]
                sqk = sb1("sqk", [128, 512], BF16)
                rsk = sb1("rsk", [128, 512], F32)
                memset("dve", BDo[:], 0.0, w=["BDo"])
                memset("dve", BDo[0:64, 0:64], 1.0 / 64, w=["BDo"])
                memset("dve", BDo[64:128, 64:128], 1.0 / 64, w=["BDo"])
                for X, (w1, w2, pe) in enumerate(((w_ck1, w_ck2, pe_k), (w_cv1, w_cv2, pe_v))):
                    memset("dve", BD1[X][:], 0.0, w=["BD1%d" % X])
                    memset("dve", BD2[X][:], 0.0, w=["BD2%d" % X])
                    stg, kstg = WST.next()
                    w1v = w1.rearrange("(r d) o -> d r o", d=64)
                    for h in range(2):
                        for rq in range(4):
                            dma("sp", stg[64 * h:64 * h + 64, 512 * rq:512 * rq + 512].rearrange("p (r o) -> p r o", o=64),
                                w1v[:, 8 * rq:8 * rq + 8, :], w=[kstg])
                    for h in range(2):
                        acopy(BD1[X][64 * h:64 * h + 64, :, 64 * h:64 * h + 64],
                              stg[64 * h:64 * h + 64, :].rearrange("p (r o) -> p r o", o=64), r=[kstg], w=["BD1%d" % X])
                    stg, kstg = WST.next()
                    dma("sp", stg[0:64, 0:64], w2, w=[kstg])
                    dma("sp", stg[64:128, 0:64], w2, w=[kstg])
                    dma("sp", stg[0:32, 64:128], pe, w=[kstg])
                    acopy(BD2[X][0:64, 0:64], stg[0:64, 0:64], r=[kstg], w=["BD2%d" % X])
                    acopy(BD2[X][64:128, 64:128], stg[64:128, 0:64], r=[kstg], w=["BD2%d" % X])
                    acopy(pe2[0:32, X, 0:64], stg[0:32, 64:128], r=[kstg], w=["pe2"])
                    acopy(pe2[0:32, X, 64:128], stg[0:32, 64:128], r=[kstg], w=["pe2"])
                gkc_v = g_kc.rearrange("o d -> d o")
                dma("sp", gkc[0:64, :], gkc_v, w=["gkc"])
                dma("sp", gkc[64:128, :], gkc_v, w=["gkc"])
                memset("dve", KCT[:, 511:512], 0.0, w=["KCT"])
                memset("dve", hc[1][:, 511:512], 0.0, w=["hc1"])
                memset("dve", VC[:, :, :, 64:65], 1.0, w=["VC"])
                with contextlib.ExitStack() as SPc:
                    PC = Ring(SPc, psum, "pc", [128, 512], F32, 2)
                    pb = SPc.enter_context(psum("pb", [128, 8], F32))
                    pk2 = SPc.enter_context(psum("pk2", [128, 512], F32))
                    pms = SPc.enter_context(psum("pms", [128, 512], F32))
                    pv = SPc.enter_context(psum("pv", [128, 4, 128], F32))
                    pp = SPc.enter_context(psum("pp", [128, 32], BF16))
                    for X in range(2):
                        tr(pp[:, 0:32], pe2[0:32, X, :], identb[0:32, 0:32], r=["pe2", "identb"], w=["pp"])
                        acopy(peT[:, X, :], pp[:, 0:32], r=["pp"], w=["peT"])
                        for r_ in range(32):
                            mm(pb[:, 0:1], BD1[X][:, r_, :], peT[:, X, r_:r_ + 1], r_ == 0, r_ == 31, r=["BD1%d" % X, "peT"], w=["pb"])
                        acopy(biasc[:, X:X + 1], pb[:, 0:1], r=["pb"], w=["biasc"])
                        pc, kpc = PC.next()
                        for r_ in range(32):
                            mm(pc[:, 0:511], BD1[X][:, r_, :], KVCT[:, X, r_:r_ + 16 * 510 + 1:16], r_ == 0, r_ == 31,
                               r=["BD1%d" % X, "KVCT"], w=[kpc])
                        act(hc[X][:, 0:511], pc[:, 0:511], AF.Gelu_apprx_tanh, r=[kpc, "biasc"], w=["hc%d" % X], bias=biasc[:, X:X + 1])
                    mm(pk2[:, 0:511], BD2[0][:], hc[0][:, 0:511], True, True, r=["BD20", "hc0"], w=["pk2"])
                    act(sqk[:, 0:511], pk2[:, 0:511], AF.Square, r=["pk2"], w=["sqk"])
                    mm(pms[:, 0:511], BDo[:], sqk[:, 0:511], True, True, r=["BDo", "sqk"], w=["pms"])
                    act(rsk[:, 0:511], pms[:, 0:511], AF.Sqrt, r=["pms", "eps"], w=["rsk"], bias=epsc[:, 0:1], scale=1.0)
                    recip(rsk[:, 0:511], rsk[:, 0:511], r=["rsk"], w=["rsk"])
                    stt(KCT[:, 0:511], pk2[:, 0:511], gkc[:, 0:1], rsk[:, 0:511], ALU.mult, ALU.mult, r=["pk2", "gkc", "rsk"], w=["KCT"])
                    for ci in range(4):
                        mm(pv[:, ci, :], hc[1][:, ci * 128:(ci + 1) * 128], BD2[1][:], True, True, r=["hc1", "BD21"], w=["pv"])
                    acopy(VC[:, :, :, 0:64], pv[:].rearrange("p c (k d) -> p c k d", d=64), r=["pv"], w=["VC"])
                    dbg_out(SPc, KCT[:], [128, 512], ["KCT"])
                    dbg_out(SPc, VC[:].rearrange("p a b c -> p (a b c)"), [128, 520], ["VC"])
                    P.flush()
            with contextlib.ExitStack() as SB:
                sbb = lambda n, s, d: SB.enter_context(sbuf(n, s, d))
                wuv = sbb("wuv", [128, 8, 1024], BF16)
                wsf = sbb("wsf", [128, 8, 128], F32)
                wsm = sbb("wsm", [128, 8, 128], BF16)
                wsT = sbb("wsT", [128, 8, 128], BF16)
                trilf = sbb("trilf", [128, 128], F32)
                bsf = sbb("bsf", [8, 128], F32)
                bsT = sbb("bsT", [128, 8], F32)
                lngb = sbb("lngb", [128, 512], F32)
                lnbb = sbb("lnbb", [128, 512], F32)
                for k in range(8):
                    load_cast(wuv[:, k, :], w_in_v[:, k, C_U:C_U + 1024], "wuv")
                dma("sp", wsf[:], w_s.rearrange("g p r -> p g r"), w=["wsf"])
                dma("sp", trilf[:], tril, w=["trilf"])
                dma("sp", bsf[:], b_s, w=["bsf"])
                dma("sp", lngb[:], lng.partition_broadcast(128)[:, 0, :], w=["lngb"])
                dma("sp", lnbb[:], lnb.partition_broadcast(128)[:, 0, :], w=["lnbb"])
                tt(wsm[:], wsf[:], trilf[:].unsqueeze(1).to_broadcast([128, 8, 128]), ALU.mult, r=["wsf", "trilf"], w=["wsm"])
                RHT = Ring(SB, sbuf, "hTb", [128, 8, 128], BF16, 2)
                RU = Ring(SB, sbuf, "ub", [128, 512], F32, 2)
                RGV = Ring(SB, sbuf, "gvb", [128, 512], F32, 2)
                RVN = Ring(SB, sbuf, "vn", [128, 512], F32, 1)
                RVB = Ring(SB, sbuf, "vnb", [128, 512], BF16, 2)
                RSB = Ring(SB, sbuf, "sbt", [128, 512], F32, 1)
                ROA = Ring(SB, sbuf, "oab", [128, 512], BF16, 2)
                RST2 = Ring(SB, sbuf, "lst", [128, 8], F32, 2)
                with contextlib.ExitStack() as SPb:
                    PTR = Ring(SPb, psum, "ptrb", [128, 8, 128], BF16, 1)
                    PUV = Ring(SPb, psum, "puv", [128, 1024], F32, 2)
                    PSG = Ring(SPb, psum, "psg", [128, 512], F32, 1)
                    PT4 = Ring(SPb, psum, "pt4", [128, 4, 128], BF16, 1)
                    pbs = SPb.enter_context(psum("pbs", [128, 8], F32))
                    ptw, kptw = PTR.next()
                    for g in range(8):
                        tr(ptw[:, g, :], wsm[:, g, :], identb[:], r=["wsm", "identb"], w=[kptw])
                    acopy(wsT[:], ptw[:], r=[kptw], w=["wsT"])
                    tr(pbs[:, 0:8], bsf[0:8, :], identf[0:8, 0:8], r=["bsf", "identf"], w=["pbs"])
                    acopy(bsT[:], pbs[:, 0:8], r=["pbs"], w=["bsT"])
                    def gmlp_tile(n, xsrc, wsT_ap, kws, bsT_ap, kbs, oat_dst, koat, va_out):
                        xt, kx = RX.next()
                        dma("sp", xt[:n], xsrc, w=[kx])
                        hT, khT = RHT.next()
                        norm_T(xt, kx, g1b, "g1b", n, hT[:, :, :n], khT, PTR)
                        puv, kpuv = PUV.next()
                        for half in range(2):
                            for k in range(8):
                                mm(puv[:n, 512 * half:512 * half + 512], hT[:, k, :n], wuv[:, k, 512 * half:512 * half + 512],
                                   k == 0, k == 7, r=[khT, "wuv"], w=[kpuv])
                        ub, kub = RU.next()
                        gvb, kgv = RGV.next()
                        ls, kls = RST2.next()
                        act(ub[:n], puv[:n, 0:512], AF.Gelu_apprx_tanh, r=[kpuv], w=[kub])
                        act(gvb[:n], puv[:n, 512:1024], AF.Gelu_apprx_tanh, r=[kpuv], w=[kgv, kls], accum_out=ls[:n, 0:1])
                        ts(ls[:n, 1:2], ls[:n, 0:1], -1.0 / 512, None, ALU.mult, None, r=[kls], w=[kls])
                        sbt, ksb = RSB.next()
                        act(sbt[:n], gvb[:n], AF.Square, r=[kgv, kls], w=[ksb, kls], bias=ls[:n, 1:2], accum_out=ls[:n, 2:3])
                        act(ls[:n, 3:4], ls[:n, 2:3], AF.Sqrt, r=[kls, "eps"], w=[kls], bias=epsc[:n, 0:1], scale=1.0 / 512)
                        recip(ls[:n, 4:5], ls[:n, 3:4], r=[kls], w=[kls])
                        vn, kvn = RVN.next()
                        ts(vn[:n], gvb[:n], ls[:n, 1:2], ls[:n, 4:5], ALU.add, ALU.mult, r=[kgv, kls], w=[kvn])
                        tt(vn[:n], vn[:n], lngb[:n], ALU.mult, r=[kvn, "lngb"], w=[kvn])
                        tt(vn[:n], vn[:n], lnbb[:n], ALU.add, r=[kvn, "lnbb"], w=[kvn])
                        if va_out is not None:
                            dma("sp", va_out, vn[:n], r=[kvn], o=True)
                        vnb, kvb = RVB.next()
                        acopy(vnb[:n], vn[:n], r=[kvn], w=[kvb])
                        psg, kpsg = PSG.next()
                        for g in range(8):
                            mm(psg[:n, 64 * g:64 * g + 64], wsT_ap[:n, g, :n], vnb[:n, 64 * g:64 * g + 64], True, True,
                               r=[kws, kvb], w=[kpsg])
                        tt(sbt[:n].rearrange("p (g c) -> p g c", c=64), psg[:n].rearrange("p (g c) -> p g c", c=64),
                           bsT_ap[:n].unsqueeze(2).to_broadcast([n, 8, 64]), ALU.add, r=[kpsg, kbs], w=[ksb])
                        oab, koa = ROA.next()
                        tt(oab[:n], sbt[:n], ub[:n], ALU.mult, r=[ksb, kub], w=[koa])
                        pt4, kp4 = PT4.next()
                        for c in range(4):
                            tr(pt4[:, c, :n], oab[:n, 128 * c:128 * c + 128], identb[:n, :n], r=[koa, "identb"], w=[kp4])
                        acopy(oat_dst, pt4[:, :, :n], r=[kp4], w=[koat])

                    for i in range(DBG_NOWN):
                        tok = slice(128 * i, 128 * i + 128)
                        gmlp_tile(128, xo[tok, :], wsT, "wsT", bsT, "bsT", OAT[:, :, tok], "OAT", o_pva if i == NOWN - 1 else None)
                    wsTs = sbb("wsTs", [16, 8, 16], BF16)
                    bsTs = sbb("bsTs", [16, 8], F32)
                    memset("dve", wsTs[:], 0.0, w=["wsTs"])
                    for sq in range(4):
                        dma("sp", wsTs[4 * sq:4 * sq + 4, :, 4 * sq:4 * sq + 4], wsT[0:4, :, 0:4], r=["wsT", "wsTs"], w=["wsTs"])
                        dma("sp", bsTs[4 * sq:4 * sq + 4, :], bsT[0:4, :], r=["bsT"], w=["bsTs"])
                    gmlp_tile(16, xs, wsTs, "wsTs", bsTs, "bsTs", OATs[:, :, :], "OATs", o_sva)
                    P.flush()
            with contextlib.ExitStack() as SC:
                sbc = lambda n, s, d: SC.enter_context(sbuf(n, s, d))
                wq = sbc("wq", [128, 8, 536], BF16)
                gqb = sbc("gqb", [128, 8, 64], F32)
                cosO_s = sbc("cosO_s", [128, NOWN + 1, 8], F32)
                sinO_s = sbc("sinO_s", [128, NOWN + 1, 8], F32)
                msel_b = sbc("msel_b", [128, 4, 128], BF16)
                mwin_b = sbc("mwin_b", [128, 8, 128], BF16)
                mcq_b = sbc("mcq_b", [128, 32], BF16)
                mct_b = sbc("mct_b", [32, 128], BF16)
                zw_b = sbc("zw_b", [32, 288], BF16)
                m1c_s = sbc("m1c_s", [128, 256], F32)
                m2c_s = sbc("m2c_s", [128, 256], F32)
                for k in range(8):
                    load_cast(wq[:, k, 0:512], w_in_v[:, k, C_Q:C_Q + 512], "wq")
                    load_cast(wq[:, k, 512:536], w_in_v[:, k, C_G:C_G + 24], "wq")
                for h in range(8):
                    dma("sp", gqb[:, h, :], g_q.partition_broadcast(128)[:, 0, :], w=["gqb"])
                ts(gqb[:], gqb[:], 0.125, None, ALU.mult, None, r=["gqb"], w=["gqb"])
                dma("sp", cosO_s[:], cosO, w=["cosO"])
                dma("sp", sinO_s[:], sinO, w=["cosO2"])
                load_cast(msel_b[:].rearrange("p a b -> p (a b)"), msel.rearrange("p a b -> p (a b)"), "msel")
                load_cast(mwin_b[:].rearrange("p a b -> p (a b)"), mwin.rearrange("p a b -> p (a b)"), "mwin")
                load_cast(mcq_b[:], mcq, "mcq")
                load_cast(mct_b[:], mct, "mct", 0, 32)
                load_cast(zw_b[:], zw, "zw", 0, 32)
                dma("sp", m1c_s[:], m1c, w=["m1c"])
                dma("sp", m2c_s[:], m2c, w=["m2c"])
                RHT = Ring(SC, sbuf, "hTc", [128, 8, 128], BF16, 2)
                sqt = sbc("sqt", [128, 512], F32)
                qf = sbc("qf", [128, 8, 64], F32)
                qnb = sbc("qnb", [128, 512], BF16)
                qrb = sbc("qrb", [128, 8, 64], BF16)
                QNT = sbc("QNT", [128, 4, 128], BF16)
                R0 = sbc("R0", [128, 2, 4, 128], BF16)
                R1 = sbc("R1", [128, 2, 4, 128], BF16)
                r3 = sbc("r3", [128, 8, 2, 8], F32)
                r4 = sbc("r4", [128, 8, 2, 8], F32)
                qs = sbc("qs", [128, 24], F32)
                sg = sbc("sg", [128, 24], F32)
                RE = Ring(SC, sbuf, "Eg", [128, 512], F32, 2)
                imp = sbc("imp", [128, 512], F32)
                Ab = sbc("Ab", [128, 128], F32)
                blk = sbc("blk", [128, 128], F32)
                rank = sbc("rank", [128, 128], F32)
                work = sbc("work", [128, 128], F32)
                top = sbc("top", [128, 16], F32)
                rs = sbc("rs", [128, 12], F32)
                Wn = [sbc("Wn0", [128, 192], BF16), sbc("Wn1", [128, 192], BF16)]
                RPT = Ring(SC, sbuf, "PT", [128, 512], BF16, 3)
                ob = sbc("ob", [128, 2, 4, 64], F32)
                obt = sbc("obt", [128, 4, 64], F32)
                obb = sbc("obb", [128, 512], BF16)
                RCS = Ring(SC, sbuf, "cst", [128, 12], F32, 2)
                zb = sbc("zb", [128, 512], BF16)
                memset("dve", zb[:], 0.0, w=["zb"])
                memset("dve", imp[:], 0.0, w=["imp"])
                memset("dve", Wn[0][:], 0.0, w=["Wn0"])
                memset("dve", Wn[1][:], 0.0, w=["Wn1"])
                Rk = [R0, R1]
                KE = [KE0, KE1]
                with contextlib.ExitStack() as SPc2:
                    PTR = Ring(SPc2, psum, "ptrc", [128, 8, 128], BF16, 1)
                    PPQ = Ring(SPc2, psum, "ppq", [128, 1024], F32, 1)
                    PST = Ring(SPc2, psum, "pst", [128, 512], F32, 3)
                    PO = Ring(SPc2, psum, "po", [128, 4, 128], F32, 2)

                    def pv_accum(po, kpo, PTt, kPT, M, vrhs, vkey, first, last):
                        if first:
                            mm(po[:].rearrange("p g c -> p (g c)"), zb[:, 0:128], zb[:, :], True, False, r=["zb"], w=[kpo])
                        for g in range(4):
                            mm(po[:, g, 0:65], PTt[:M, 128 * g:128 * g + 128], vrhs, False, last and g == 3, r=[kPT, vkey], w=[kpo])

                    def combine(po, kpo, kv, br):
                        cs, kcs_ = RCS.next()
                        ts(cs[:, 0:4], po[:, :, 64], 1e-30, None, ALU.max, None, r=[kpo], w=[kcs_])
                        recip(cs[:, 4:8], cs[:, 0:4], r=[kcs_], w=[kcs_])
                        tt(cs[:, 8:12], cs[:, 4:8], sg[:, kv * 12 + br:kv * 12 + 12:3], ALU.mult, r=[kcs_, "sg"], w=[kcs_])
                        cb = cs[:, 8:12].unsqueeze(2).to_broadcast([128, 4, 64])
                        if br == 0:
                            tt(ob[:, kv], po[:, :, 0:64], cb, ALU.mult, r=[kpo, kcs_], w=["ob"])
                        else:
                            tt(obt[:], po[:, :, 0:64], cb, ALU.mult, r=[kpo, kcs_], w=["obt"])
                            tt(ob[:, kv], ob[:, kv], obt[:], ALU.add, r=["ob", "obt"], w=["ob"])

                    def q_part(n, xsrc, icos, sample):
                        xt, kx = RX.next()
                        dma("sp", xt[:n], xsrc, w=[kx])
                        hT, khT = RHT.next()
                        norm_T(xt, kx, g1b, "g1b", n, hT[:, :, :n], khT, PTR)
                        pq, kpq = PPQ.next()
                        for k in range(8):
                            mm(pq[:n, 0:512], hT[:, k, :n], wq[:, k, 0:512], k == 0, k == 7, r=[khT, "wq"], w=[kpq])
                        for k in range(8):
                            mm(pq[:n, 512:536], hT[:, k, :n], wq[:, k, 512:536], k == 0, k == 7, r=[khT, "wq"], w=[kpq])
                        act(sg[:n, 0:24], pq[:n, 512:536], AF.Sigmoid, r=[kpq], w=["sg"])
                        act(sqt[:n], pq[:n, 0:512], AF.Square, r=[kpq], w=["sqt"])
                        P.op("dve", lambda e: e.tensor_reduce(out=qs[:n, 0:8], in_=sqt[:n].rearrange("p (a d) -> p a d", d=64),
                                                             axis=AX.X, op=ALU.add), r=["sqt"], w=["qs"])
                        act(qs[:n, 8:16], qs[:n, 0:8], AF.Sqrt, r=["qs", "eps"], w=["qs"], bias=epsc[:n, 0:1], scale=1.0 / 64)
                        recip(qs[:n, 16:24], qs[:n, 8:16], r=["qs"], w=["qs"])
                        tt(qf[:n], pq[:n, 0:512].rearrange("p (a d) -> p a d", d=64),
                           qs[:n, 16:24].unsqueeze(2).to_broadcast([n, 8, 64]), ALU.mult, r=[kpq, "qs"], w=["qf"])
                        tt(qf[:n], qf[:n], gqb[:n], ALU.mult, r=["qf", "gqb"], w=["qf"])
                        acopy(qnb[:n], qf[:n].rearrange("p a d -> p (a d)"), r=["qf"], w=["qnb"])
                        X = qf[:n, :, 0:16].rearrange("p a (h e) -> p a h e", e=8)
                        cb_ = cosO_s[:n, icos, :].unsqueeze(1).unsqueeze(1).to_broadcast([n, 8, 2, 8])
                        sb_ = sinO_s[:n, icos, :].unsqueeze(1).to_broadcast([n, 8, 8])
                        tt(r3[:n], X, cb_, ALU.mult, r=["qf", "cosO", "cosO2"], w=["r3"])
                        tt(r4[:n, :, 0, :], X[:, :, 1, :], sb_, ALU.mult, r=["qf", "cosO", "cosO2"], w=["r4"])
                        tt(r4[:n, :, 1, :], X[:, :, 0, :], sb_, ALU.mult, r=["qf", "cosO", "cosO2"], w=["r4"])
                        acopy(qrb[:n], qf[:n], r=["qf"], w=["qrb"])
                        q8 = qrb[:n, :, 0:16].rearrange("p a (h e) -> p a h e", e=8)
                        tt(q8[:, :, 0, :], r3[:n, :, 0, :], r4[:n, :, 0, :], ALU.subtract, r=["r3", "r4"], w=["qrb"])
                        tt(q8[:, :, 1, :], r3[:n, :, 1, :], r4[:n, :, 1, :], ALU.add, r=["r3", "r4"], w=["qrb"])
                        qrb2 = qrb[:].rearrange("p a d -> p (a d)")
                        for src, ksrc, which in ((qnb, "qnb", 0), (qrb2, "qrb", 1)):
                            ptq, kptq = PTR.next()
                            for g in range(4):
                                tr(ptq[:, g, :n], src[:n, (3 + g) * 64:(5 + g) * 64], identb[:n, :n], r=[ksrc, "identb"], w=[kptq])
                            for g in range(4):
                                tr(ptq[0:64, g, :n], src[:n, g * 64:(g + 1) * 64], identb[:n, :n], r=[ksrc, "identb"], w=[kptq])
                            if sample:
                                acopy((QNTs if which == 0 else QRTs)[:, :, :], ptq[:, 0:4, :n], r=[kptq], w=["QNTs" if which == 0 else "QRTs"])
                            elif which == 0:
                                acopy(QNT[:], ptq[:, 0:4, :], r=[kptq], w=["QNT"])
                            else:
                                acopy(R0[0:64], ptq[0:64, 0:4, :].unsqueeze(1).to_broadcast([64, 2, 4, 128]), r=[kptq], w=["R0"])
                                acopy(R1[64:128], ptq[64:128, 0:4, :].unsqueeze(1).to_broadcast([64, 2, 4, 128]), r=[kptq], w=["R1"])
                        if sample:
                            pgt, kpgt = PST.next()
                            tr(pgt[0:24, 0:16], sg[:16, 0:24], identf[:16, :16], r=["sg", "identf"], w=[kpgt])
                            acopy(sgTs[:, :], pgt[0:24, 0:16], r=[kpgt], w=["sgTs"])

                    for i in range(DBG_NOWN + 1):
                        if i == DBG_NOWN:
                            q_part(16, xs, NOWN, True)
                            break
                        tok = slice(128 * i, 128 * i + 128)
                        q_part(128, xo[tok, :], i, False)
                        ncb = min(32 * i + 31, 511)
                        lo = max(32 * i - 1, 0)
                        a0 = 1 if i == 0 else 0
                        for kv in range(2):
                            kr = slice(64 * kv, 64 * kv + 64)
                            for g in range(4):
                                ps, kps = PST.next()
                                if lo > 0:
                                    mm(ps[:, 0:lo], QNT[kr, g, :], KCT[kr, 0:lo], True, True, r=["QNT", "KCT"], w=[kps])
                                mm(ps[:, lo:ncb], QNT[kr, g, :], KCT[kr, lo:ncb], True, False, r=["QNT", "KCT"], w=[kps])
                                mm(ps[:, lo:ncb], identb[:], mcq_b[:, a0:a0 + ncb - lo], False, True, r=["identb", "mcq"], w=[kps])
                                Eg, kE = RE.next()
                                act(Eg[:, 0:ncb], ps[:, 0:ncb], AF.Exp, r=[kps], w=[kE, "rs"], accum_out=rs[:, g:g + 1])
                                ts(rs[:, 4 + g:5 + g], rs[:, g:g + 1], 1e-30, None, ALU.max, None, r=["rs"], w=["rs"])
                                recip(rs[:, 8 + g:9 + g], rs[:, 4 + g:5 + g], r=["rs"], w=["rs"])
                                if g == 0:
                                    ts(imp[:, 0:ncb], Eg[:, 0:ncb], rs[:, 8:9], None, ALU.mult, None, r=[kE, "rs"], w=["imp"])
                                else:
                                    stt(imp[:, 0:ncb], Eg[:, 0:ncb], rs[:, 8 + g:9 + g], imp[:, 0:ncb], ALU.mult, ALU.add,
                                        r=[kE, "rs", "imp"], w=["imp"])
                            P.op("dve", lambda e: e.tensor_reduce(out=Ab[:], in_=imp[:].rearrange("p (j f) -> p j f", f=4),
                                                                 axis=AX.X, op=ALU.add), r=["imp"], w=["Ab"])
                            imp3 = imp[:, 3::4]
                            stt(blk[:], Ab[:], 2.0, imp3, ALU.mult, ALU.subtract, r=["Ab", "imp"], w=["blk"])
                            tt(blk[:, 1:128], blk[:, 1:128], imp[:, 3:508:4], ALU.add, r=["blk", "imp"], w=["blk"])
                            off = 128 - 8 * i
                            tt(rank[:], blk[:], m1c_s[:, off:off + 128], ALU.mult, r=["blk", "m1c"], w=["rank"])
                            tt(rank[:], rank[:], m2c_s[:, off:off + 128], ALU.add, r=["rank", "m2c"], w=["rank"])
                            memset("dve", rank[:, 0:1], 1e4, w=["rank"])
                            P.op("dve", lambda e: e.max(out=top[:, 0:8], in_=rank[:]), r=["rank"], w=["top"])
                            P.op("dve", lambda e: e.match_replace(out=work[:], in_to_replace=top[:, 0:8], in_values=rank[:],
                                                                 imm_value=-1e30), r=["rank", "top"], w=["work"])
                            P.op("dve", lambda e: e.max(out=top[:, 8:16], in_=work[:]), r=["work"], w=["top"])
                            kW = "Wn%d" % kv
                            ts(Wn[kv][:, 64:192], rank[:], top[:, 15:16], NEGM, ALU.is_lt, ALU.mult, r=["rank", "top"], w=[kW])
                            ptn, kptn = PTR.next()
                            if kv == 0:
                                tr(ptn[:, 0, :], Wn[0][:, 0:128], identb[:], r=[kW, "identb"], w=[kptn])
                                tr(ptn[:, 1, :], Wn[0][:, 64:192], identb[:], r=[kW, "identb"], w=[kptn])
                                acopy(R0[64:128], ptn[64:128, 0:2, :].unsqueeze(2).to_broadcast([64, 2, 4, 128]), r=[kptn], w=["R0"])
                            else:
                                tr(ptn[0:64, 0, :], Wn[1][:, 64:128], identb[:], r=[kW, "identb"], w=[kptn])
                                tr(ptn[0:64, 1, :], Wn[1][:, 128:192], identb[:], r=[kW, "identb"], w=[kptn])
                                acopy(R1[0:64], ptn[0:64, 0:2, :].unsqueeze(2).to_broadcast([64, 2, 4, 128]), r=[kptn], w=["R1"])
                            kR = "R%d" % kv
                            Rt = Rk[kv]
                            qn_rhs = QNT[kr, :, :]
                            qr_rhs = Rt[kr, 0, :, :]
                            po, kpo = PO.next()
                            nct = (ncb + 127) // 128
                            c0 = 32 * i - 1
                            for ci in range(nct):
                                M = min(128, ncb - 128 * ci)
                                base = c0 - 128 * ci
                                masked = (base + 31 >= 0) and (base < M)
                                ps, kps = PST.next()
                                ps3 = ps[:].rearrange("p (g q) -> p g q", g=4)
                                mm(ps3[:M], KCT[kr, 128 * ci:128 * ci + M], qn_rhs, True, not masked, r=["KCT", "QNT"], w=[kps])
                                if masked:
                                    mm(ps3[:M], zw_b[0:32, 128 - base:128 - base + M],
                                       mct_b[0:32, :].unsqueeze(1).to_broadcast([32, 4, 128]), False, True, r=["zw", "mct"], w=[kps])
                                PTt, kPT = RPT.next()
                                act(PTt[:M, :], ps[:M, :], AF.Exp, r=[kps], w=[kPT])
                                pv_accum(po, kpo, PTt, kPT, M, VC[:M, ci, kv, 0:65], "VC", ci == 0, ci == nct - 1)
                            combine(po, kpo, kv, 0)
                            po, kpo = PO.next()
                            nk = min(4 * i + 4, DBG_NT)
                            for kt in range(nk):
                                x_ = 0 if kt < 32 else 1
                                m_ = kt - 4 * i
                                ps, kps = PST.next()
                                ps3 = ps[:].rearrange("p (g q) -> p g q", g=4)
                                mm(ps3, KE[kv][:, 128 * kt:128 * kt + 128], Rt[:, x_, :, :], True, m_ < 0, r=["KE%d" % kv, kR], w=[kps])
                                if m_ >= 0:
                                    mm(ps3, identb[:], msel_b[:, m_, :].unsqueeze(1).to_broadcast([128, 4, 128]), False, True,
                                       r=["identb", "msel"], w=[kps])
                                PTt, kPT = RPT.next()
                                act(PTt[:], ps[:], AF.Exp, r=[kps], w=[kPT])
                                pv_accum(po, kpo, PTt, kPT, 128, VS[:, kt, kv, 0:65], "VS", kt == 0, kt == nk - 1)
                            combine(po, kpo, kv, 1)
                            po, kpo = PO.next()
                            kts = [kt for kt in range(4 * i - 4, 4 * i + 4) if 0 <= kt < DBG_NT]
                            for n_, kt in enumerate(kts):
                                m_ = kt - (4 * i - 4)
                                ps, kps = PST.next()
                                ps3 = ps[:].rearrange("p (g q) -> p g q", g=4)
                                mm(ps3, KWT[kr, 128 * kt:128 * kt + 128], qr_rhs, True, False, r=["KWT", kR], w=[kps])
                                mm(ps3, identb[:], mwin_b[:, m_, :].unsqueeze(1).to_broadcast([128, 4, 128]), False, True,
                                   r=["identb", "mwin"], w=[kps])
                                PTt, kPT = RPT.next()
                                act(PTt[:], ps[:], AF.Exp, r=[kps], w=[kPT])
                                pv_accum(po, kpo, PTt, kPT, 128, VW[:, kt, kv, 0:65], "VW", n_ == 0, n_ == len(kts) - 1)
                            combine(po, kpo, kv, 2)
                        acopy(obb[:], ob[:].rearrange("p k g d -> p (k g d)"), r=["ob"], w=["obb"])
                        pt4, kp4 = PTR.next()
                        for c in range(4):
                            tr(pt4[:, c, :], obb[:, 128 * c:128 * c + 128], identb[:], r=["obb", "identb"], w=[kp4])
                        acopy(OBT[:, :, tok], pt4[:, 0:4, :], r=[kp4], w=["OBT"])
                    dbg_out(SPc2, OBT[:, :, 256:512], [128, 4, 256], ["OBT"])
                    P.flush()
        if DBG_SAMPLE:
          with contextlib.ExitStack() as SS:
            sbs = lambda n, s, d: SS.enter_context(sbuf(n, s, d))
            PGa = sbs("PGa", [128, 16384], BF16)
            PGb = sbs("PGb", [128, 16384], BF16)
            XT = PGb
            BD1s = sbs("BD1s", [128, 32, 128], BF16)
            BD2s = [sbs("BD2sk", [128, 128], BF16), sbs("BD2sv", [128, 128], BF16)]
            BDos = sbs("BDos", [128, 128], BF16)
            pe2s = sbs("pe2s", [32, 2, 128], BF16)
            peTs = sbs("peTs", [128, 2, 32], BF16)
            gkcs = sbs("gkcs", [128, 1], F32)
            biass = sbs("biass", [128, 2], F32)
            hcs = sbs("hcs", [128, 1024], BF16)
            sqs = sbs("sqs", [128, 1024], BF16)
            rss = sbs("rss", [128, 1024], F32)
            KCTs = sbs("KCTs", [128, 1024], BF16)
            VCs = sbs("VCs", [128, 8, 2, 64], BF16)
            idx = sbs("idx", [128, 4], I32)
            ones_b = sbs("ones_b", [128, 128], BF16)
            zbs = sbs("zbs", [128, 128], BF16)
            oh_b = sbs("oh_b", [24, 24, 128], BF16)
            mnew_s = sbs("mnew_s", [128, 4, 16], F32)
            mw4_s = sbs("mw4_s", [128, 4, 16], F32)
            cmask_s = sbs("cmask_s", [128, 8], F32)
            RKT = Ring(SS, sbuf, "KTb", [128, 8, 128], BF16, 2)
            PTc32 = sbs("PTc32", [128, 8, 16], F32)
            PTcb = sbs("PTcb", [128, 8, 16], BF16)
            Pn = sbs("Pn", [128, 8, 16], F32)
            rDc = sbs("rDc", [128, 16], F32)
            impT = sbs("impT", [128, 8, 4], F32)
            impS = sbs("impS", [4, 1028], F32)
            As = sbs("As", [4, 257], F32)
            blks = sbs("blks", [4, 257], F32)
            works = sbs("works", [4, 257], F32)
            tops = sbs("tops", [4, 16], F32)
            selm = sbs("selm", [4, 258], F32)
            MSK = sbs("MSK", [128, 2, 2, 4], F32)
            G2s = sbs("G2s", [128, 3, 2, 16], F32)
            RE32 = Ring(SS, sbuf, "E32", [128, 256], F32, 2)
            RPTm = Ring(SS, sbuf, "PTm", [128, 256], BF16, 2)
            PTn = sbs("PTn", [128, 2, 16], BF16)
            En = sbs("En", [128, 2, 16], F32)
            kwf = sbs("kwf", [128, 4, 128], F32)
            kwb = sbs("kwb", [128, 4, 128], BF16)
            KWTs = sbs("KWTs", [128, 4, 128], BF16)
            VWs = sbs("VWs", [128, 4, 128], BF16)
            tot = sbs("tot", [128, 2, 16], F32)
            tmp16 = sbs("tmp16", [128, 2, 16], F32)
            rD = sbs("rD", [128, 2, 16], F32)
            totb = sbs("totb", [64, 32], BF16)
            II = sbs("II", [64, 128], BF16)
            acopy(II[:, 0:64], identf[0:64, 0:64], r=["identf"], w=["II"])
            acopy(II[:, 64:128], identf[0:64, 0:64], r=["identf"], w=["II"])
            dma("sp", idx[:], pt, w=["idx"])
            memset("dve", ones_b[:], 1.0, w=["ones_b"])
            memset("dve", zbs[:], 0.0, w=["zbs"])
            memset("dve", BDos[:], 0.0, w=["BDos"])
            memset("dve", BDos[0:64, 0:64], 1.0 / 64, w=["BDos"])
            memset("dve", BDos[64:128, 64:128], 1.0 / 64, w=["BDos"])
            memset("dve", impS[:], 0.0, w=["impS"])
            memset("dve", KCTs[:, 1023:1024], 0.0, w=["KCTs"])
            memset("dve", hcs[:, 1023:1024], 0.0, w=["hcs"])
            load_cast(oh_b[:].rearrange("p a b -> p (a b)"), oh.rearrange("p a b -> p (a b)"), "oh_b", 0, 24)
            dma("sp", mnew_s[:], mnew, w=["mnew"])
            dma("sp", mw4_s[:], mw4, w=["mw4"])
            dma("sp", cmask_s[:], cmask, w=["cmask"])
            gkc_v = g_kc.rearrange("o d -> d o")
            dma("sp", gkcs[0:64, :], gkc_v, w=["gkcs"])
            dma("sp", gkcs[64:128, :], gkc_v, w=["gkcs"])
            for X, (w2, pe) in enumerate(((w_ck2, pe_k), (w_cv2, pe_v))):
                memset("dve", BD2s[X][:], 0.0, w=["BD2s%d" % X])
                stg, kstg = WST.next()
                dma("sp", stg[0:64, 0:64], w2, w=[kstg])
                dma("sp", stg[64:128, 0:64], w2, w=[kstg])
                dma("sp", stg[0:32, 64:128], pe, w=[kstg])
                acopy(BD2s[X][0:64, 0:64], stg[0:64, 0:64], r=[kstg], w=["BD2s%d" % X])
                acopy(BD2s[X][64:128, 64:128], stg[64:128, 0:64], r=[kstg], w=["BD2s%d" % X])
                acopy(pe2s[0:32, X, 0:64], stg[0:32, 64:128], r=[kstg], w=["pe2s"])
                acopy(pe2s[0:32, X, 64:128], stg[0:32, 64:128], r=[kstg], w=["pe2s"])

            def load_bd1(w1):
                memset("dve", BD1s[:], 0.0, w=["BD1s"])
                stg, kstg = WST.next()
                w1v = w1.rearrange("(r d) o -> d r o", d=64)
                for h in range(2):
                    for rq in range(4):
                        dma("sp", stg[64 * h:64 * h + 64, 512 * rq:512 * rq + 512].rearrange("p (r o) -> p r o", o=64),
                            w1v[:, 8 * rq:8 * rq + 8, :], w=[kstg])
                for h in range(2):
                    acopy(BD1s[64 * h:64 * h + 64, :, 64 * h:64 * h + 64],
                          stg[64 * h:64 * h + 64, :].rearrange("p (r o) -> p r o", o=64), r=[kstg], w=["BD1s"])

            with contextlib.ExitStack() as SPs:
                PTB = Ring(SPs, psum, "ptbs", [128, 8, 128], BF16, 1)
                PXs = Ring(SPs, psum, "pxs", [128, 512], F32, 3)
                ACC = [SPs.enter_context(psum("acc%d" % a_, [128, 512], F32)) for a_ in range(4)]
                acc_open = [False] * 4

                def acc_mm(a_, lhsT, rhs, last, r):
                    mm(ACC[a_][0:64, 0:16], lhsT, rhs, not acc_open[a_], last, r=r, w=["acc%d" % a_])
                    acc_open[a_] = not last
                XTv = XT[:].rearrange("p (q r) -> p q r", r=128)

                def gather(dst, kdst, pool, sq):
                    P.flush()
                    P.op("pool", lambda e: e.indirect_dma_start(out=dst[:, :], out_offset=None, in_=pool[:, :],
                         in_offset=bass.IndirectOffsetOnAxis(ap=idx[:, sq:sq + 1], axis=0)), r=(), w=[kdst], dma=True)
                    P.flush()
                    for _d in range(DBG_DELAY):
                        acopy(hcs[:, :], hcs[:, :], r=["hcs"], w=["hcs"])
                    if DBG_DELAY:
                        P.flush()

                def to_XT(src, ksrc):
                    for rb in range(16):
                        ptb, kptb = PTB.next()
                        for j in range(8):
                            r_ = 8 * rb + j
                            tr(ptb[:, j, :], src[:, 128 * r_:128 * r_ + 128], identb[:], r=[ksrc, "identb"], w=[kptb])
                        acopy(XTv[:, :, 8 * rb:8 * rb + 8], ptb[:].rearrange("p r q -> p q r"), r=[kptb], w=["PGb"])

                QR = [sbs("QR0", [128, 16], BF16), sbs("QR1", [128, 16], BF16)]
                QN = [sbs("QN0", [128, 16], BF16), sbs("QN1", [128, 16], BF16)]
                for t_ in QR + QN:
                    memset("dve", t_[:], 0.0, w=[t_.name if hasattr(t_, "name") else "qz"])
                P.flush()
                for sq in range(DBG_NSEQ):
                    qcols = slice(4 * sq, 4 * sq + 4)
                    for kv in range(2):
                        kr_ = slice(64 * kv, 64 * kv + 64)
                        acopy(QR[kv][kr_, :].rearrange("p (g t) -> p g t", g=4), QRTs[kr_, :, qcols], r=["QRTs"], w=["QRz"])
                        acopy(QN[kv][kr_, :].rearrange("p (g t) -> p g t", g=4), QNTs[kr_, :, qcols], r=["QNTs"], w=["QRz"])
                    for X, (pool_, w1) in enumerate(((pkc, w_ck1), (pvc, w_cv1))):
                        load_bd1(w1)
                        gather(PGa, "PGa", pool_, sq)
                        to_XT(PGa, "PGa")
                        if sq == 0:
                            ptb, kptb = PTB.next()
                            tr(ptb[:, 0, 0:32], pe2s[0:32, X, :], identb[0:32, 0:32], r=["pe2s", "identb"], w=[kptb])
                            acopy(peTs[:, X, :], ptb[:, 0, 0:32], r=[kptb], w=["peTs"])
                        pb_, kpb_ = PXs.next()
                        for r_ in range(32):
                            mm(pb_[:, 0:1], BD1s[:, r_, :], peTs[:, X, r_:r_ + 1], r_ == 0, r_ == 31, r=["BD1s", "peTs"], w=[kpb_])
                        acopy(biass[:, X:X + 1], pb_[:, 0:1], r=[kpb_], w=["biass"])
                        for nt_ in range(2):
                            c0, c1 = (0, 512) if nt_ == 0 else (512, 1023)
                            pc, kpc = PXs.next()
                            for r_ in range(32):
                                mm(pc[:, 0:c1 - c0], BD1s[:, r_, :], XT[:, 16 * c0 + r_:16 * (c1 - 1) + r_ + 1:16], r_ == 0, r_ == 31,
                                   r=["BD1s", "PGb"], w=[kpc])
                            act(hcs[:, c0:c1], pc[:, 0:c1 - c0], AF.Gelu_apprx_tanh, r=[kpc, "biass"], w=["hcs"], bias=biass[:, X:X + 1])
                        if X == 0:
                            for nt_ in range(2):
                                c0, c1 = (0, 512) if nt_ == 0 else (512, 1023)
                                n_ = c1 - c0
                                pk2, kpk2 = PXs.next()
                                mm(pk2[:, 0:n_], BD2s[0][:], hcs[:, c0:c1], True, True, r=["BD2s0", "hcs"], w=[kpk2])
                                act(sqs[:, c0:c1], pk2[:, 0:n_], AF.Square, r=[kpk2], w=["sqs"])
                                pms, kpms = PXs.next()
                                mm(pms[:, 0:n_], BDos[:], sqs[:, c0:c1], True, True, r=["BDos", "sqs"], w=[kpms])
                                act(rss[:, c0:c1], pms[:, 0:n_], AF.Sqrt, r=[kpms, "eps"], w=["rss"], bias=epsc[:, 0:1], scale=1.0)
                                recip(rss[:, c0:c1], rss[:, c0:c1], r=["rss"], w=["rss"])
                                stt(KCTs[:, c0:c1], pk2[:, 0:n_], gkcs[:, 0:1], rss[:, c0:c1], ALU.mult, ALU.mult,
                                    r=[kpk2, "gkcs", "rss"], w=["KCTs"])
                        else:
                            for half in range(2):
                                pv_, kpv_ = PXs.next()
                                for cj in range(4):
                                    ci = 4 * half + cj
                                    mm(pv_[:, 128 * cj:128 * cj + 128], hcs[:, 128 * ci:128 * ci + 128], BD2s[1][:], True, True,
                                       r=["hcs", "BD2s1"], w=[kpv_])
                                acopy(VCs[:, 4 * half:4 * half + 4, :, :], pv_[:].rearrange("p (c k d) -> p c k d", c=4, k=2), r=[kpv_], w=["VCs"])
                    gather(PGa, "PGa", pks, sq)
                    gather(PGb, "PGb", pvs, sq)
                    pG, kpG = PXs.next()
                    for br in range(3):
                        for kv in range(2):
                            for g in range(4):
                                o_ = (br * 2 + kv) * 16 + 4 * g
                                mm(pG[0:64, o_:o_ + 4], oh_b[0:24, kv * 12 + 3 * g + br, 0:64], sgTs[0:24, qcols], True, True, r=["oh_b", "sgTs"], w=[kpG])
                    acopy(G2s[0:64].rearrange("p a b c -> p (a b c)"), pG[0:64, 0:96], r=[kpG], w=["G2s"])
                    for kv in range(2):
                        kr = slice(64 * kv, 64 * kv + 64)
                        psc, kpsc = PXs.next()
                        for ci in range(8):
                            mm(psc[:, 16 * ci:16 * ci + 16], KCTs[:, 128 * ci:128 * ci + 128], QN[kv][:, :], True, True,
                               r=["KCTs", "QRz"], w=[kpsc])
                        act(PTc32[:].rearrange("p a b -> p (a b)"), psc[:, 0:128], AF.Exp, r=[kpsc], w=["PTc32"])
                        tt(PTc32[:], PTc32[:], cmask_s[:].unsqueeze(2).to_broadcast([128, 8, 16]), ALU.mult, r=["PTc32", "cmask"], w=["PTc32"])
                        acopy(PTcb[:], PTc32[:], r=["PTc32"], w=["PTcb"])
                        pD, kpD = PXs.next()
                        for ci in range(8):
                            mm(pD[:, 0:16], ones_b[:], PTcb[:, ci, :], ci == 0, ci == 7, r=["ones_b", "PTcb"], w=[kpD])
                        recip(rDc[:], pD[:, 0:16], r=[kpD], w=["rDc"])
                        tt(Pn[:], PTc32[:], rDc[:].unsqueeze(1).to_broadcast([128, 8, 16]), ALU.mult, r=["PTc32", "rDc"], w=["Pn"])
                        P.op("dve", lambda e: e.tensor_reduce(out=impT[:], in_=Pn[:].rearrange("p c (g t) -> p c t g", g=4),
                                                             axis=AX.X, op=ALU.add), r=["Pn"], w=["impT"])
                        for half in range(2):
                            pim, kpim = PXs.next()
                            for cj in range(4):
                                ci = 4 * half + cj
                                tr(pim[0:4, 128 * cj:128 * cj + 128], impT[:, ci, :], identf[:], r=["impT", "identf"], w=[kpim])
                            acopy(impS[0:4, 512 * half:512 * half + 512], pim[0:4, 0:512], r=[kpim], w=["impS"])
                        P.op("dve", lambda e: e.tensor_reduce(out=As[:], in_=impS[:].rearrange("p (j f) -> p j f", f=4),
                                                             axis=AX.X, op=ALU.add), r=["impS"], w=["As"])
                        stt(blks[:], As[:], 2.0, impS[:, 3:1028:4], ALU.mult, ALU.subtract, r=["As", "impS"], w=["blks"])
                        tt(blks[:, 1:257], blks[:, 1:257], impS[:, 3:1024:4], ALU.add, r=["blks", "impS"], w=["blks"])
                        memset("dve", blks[:, 0:1], 1e4, w=["blks"])
                        memset("dve", blks[:, 255:257], 1e4, w=["blks"])
                        P.op("dve", lambda e: e.max(out=tops[:, 0:8], in_=blks[:]), r=["blks"], w=["tops"])
                        P.op("dve", lambda e: e.match_replace(out=works[:], in_to_replace=tops[:, 0:8], in_values=blks[:],
                                                             imm_value=-1e30), r=["blks", "tops"], w=["works"])
                        P.op("dve", lambda e: e.max(out=tops[:, 8:16], in_=works[:]), r=["works"], w=["tops"])
                        ts(selm[:, 0:257], blks[:], tops[:, 15:16], None, ALU.is_ge, None, r=["blks", "tops"], w=["selm"])
                        psm, kpsm = PXs.next()
                        for h in range(2):
                            tr(psm[:, 4 * h:4 * h + 4], selm[0:4, h:256:2], identf[0:4, 0:4], r=["selm", "identf"], w=[kpsm])
                        acopy(MSK[:, :, kv, :], psm[:, 0:8].rearrange("p (h t) -> p h t", t=4), r=[kpsm], w=["MSK"])
                        pO, kpO = PXs.next()
                        for ci in range(8):
                            mm(pO[0:64, 0:16], VCs[:, ci, kv, :], PTcb[:, ci, :], ci == 0, ci == 7,
                               r=["VCs", "PTcb"], w=[kpO])
                        tt(tmp16[0:64, kv, :], pO[0:64, 0:16], rDc[0:64], ALU.mult, r=[kpO, "rDc"], w=["tmp16"])
                        tt(tot[0:64, kv, :], tmp16[0:64, kv, :], G2s[0:64, 0, kv, :], ALU.mult, r=["tmp16", "G2s"], w=["tot"])
                    for rb in range(DBG_RB0, DBG_RB1):
                        h = rb // 8
                        ptb, kptb = PTB.next()
                        for j in range(8):
                            r_ = 8 * rb + j
                            tr(ptb[:, j, :], PGa[:, 128 * r_:128 * r_ + 128], identb[:], r=["PGa", "identb"], w=[kptb])
                        KTb, kKT = RKT.next()
                        acopy(KTb[:], ptb[:], r=[kptb], w=[kKT])
                        if DBG_BAR:
                            P.flush()
                        E32, kE32 = RE32.next()
                        for kv in range(2):
                            pss, kpss = PXs.next()
                            for j in range(8):
                                mm(pss[:, 16 * j:16 * j + 16], KTb[:, j, :], QR[kv][:, :], True, True,
                                   r=[kKT, "QRz"], w=[kpss])
                            if DBG_BAR:
                                P.flush()
                            act(E32[:, 128 * kv:128 * kv + 128], pss[:, 0:128], AF.Exp, r=[kpss], w=[kE32])
                            if DBG_BAR:
                                P.flush()
                        PTm, kPTm = RPTm.next()
                        tt(PTm[:].rearrange("p (k a t) -> p k a t", k=2, t=4), E32[:].rearrange("p (k a t) -> p k a t", k=2, t=4),
                           MSK[:, h, :, :].unsqueeze(2).to_broadcast([128, 2, 32, 4]), ALU.mult, r=[kE32, "MSK"], w=[kPTm])
                        if DBG_BAR:
                            P.flush()
                        for kv in range(2):
                            for j in range(8):
                                r_ = 8 * rb + j
                                o_ = (kv * 8 + j) * 16
                                acc_mm(kv, PGb[:, 128 * r_ + 64 * kv:128 * r_ + 64 * kv + 64], PTm[:, o_:o_ + 16], False, ["PGb", kPTm])
                                acc_mm(2 + kv, ones_b[:, 0:64], PTm[:, o_:o_ + 16], False, ["ones_b", kPTm])

                    def new_tokens(KNT, kK, VN, kV):
                        if DBG_BAR:
                            P.flush()
                        pn, kpn = PXs.next()
                        for kv in range(2):
                            mm(pn[:, 16 * kv:16 * kv + 16], KNT[:, :], QR[kv][:, :], True, True,
                               r=[kK, "QRz"], w=[kpn])
                        act(En[:].rearrange("p a b -> p (a b)"), pn[:, 0:32], AF.Exp, r=[kpn], w=["En"])
                        tt(PTn[:], En[:], mnew_s[:, sq, :].unsqueeze(1).to_broadcast([128, 2, 16]), ALU.mult, r=["En", "mnew"], w=["PTn"])
                        for kv in range(2):
                            acc_mm(kv, VN[:, kv, :], PTn[:, kv, :], True, [kV, "PTn"])
                            acc_mm(2 + kv, ones_b[:, 0:64], PTn[:, kv, :], True, ["ones_b", "PTn"])

                    def finish_branch(br):
                        if DBG_BAR:
                            P.flush()
                        for kv in range(2):
                            recip(rD[0:64, kv, :], ACC[2 + kv][0:64, 0:16], r=["acc%d" % (2 + kv)], w=["rD"])
                            tt(tmp16[0:64, kv, :], ACC[kv][0:64, 0:16], rD[0:64, kv, :], ALU.mult, r=["acc%d" % kv, "rD"], w=["tmp16"])
                        tt(tmp16[0:64], tmp16[0:64], G2s[0:64, br, :, :], ALU.mult, r=["tmp16", "G2s"], w=["tmp16"])
                        tt(tot[0:64], tot[0:64], tmp16[0:64], ALU.add, r=["tot", "tmp16"], w=["tot"])

                    new_tokens(KSNT, "KSNT", VSN, "VSN")
                    finish_branch(1)
                    dma("sp", kwf[:], kwin[sq].rearrange("(a p) c -> p a c", p=128), w=["kwf"])
                    acopy(kwb[:], kwf[:], r=["kwf"], w=["kwb"])
                    ptb, kptb = PTB.next()
                    for a in range(4):
                        tr(ptb[:, a, :], kwb[:, a, :], identb[:], r=["kwb", "identb"], w=[kptb])
                    acopy(KWTs[:], ptb[:, 0:4, :], r=[kptb], w=["KWTs"])
                    dma("sp", kwf[:], vwin[sq].rearrange("(a p) c -> p a c", p=128), r=["kwf"], w=["kwf"])
                    acopy(VWs[:], kwf[:], r=["kwf"], w=["VWs"])
                    if DBG_BAR:
                        P.flush()
                    psw, kpsw = PXs.next()
                    for kv in range(2):
                        for a in range(4):
                            o_ = (kv * 4 + a) * 16
                            mm(psw[:, o_:o_ + 16], KWTs[:, a, :], QR[kv][:, :], True, True,
                               r=["KWTs", "QRz"], w=[kpsw])
                    if DBG_BAR:
                        P.flush()
                    E32, kE32 = RE32.next()
                    act(E32[:, 0:128], psw[:, 0:128], AF.Exp, r=[kpsw], w=[kE32])
                    if DBG_BAR:
                        P.flush()
                    PTm, kPTm = RPTm.next()
                    tt(PTm[:, 0:128].rearrange("p (k a t) -> p k a t", k=2, t=16), E32[:, 0:128].rearrange("p (k a t) -> p k a t", k=2, t=16),
                       mw4_s[:].unsqueeze(1).to_broadcast([128, 2, 4, 16]), ALU.mult, r=[kE32, "mw4"], w=[kPTm])
                    if DBG_BAR:
                        P.flush()
                    for kv in range(2):
                        for a in range(4):
                            o_ = (kv * 4 + a) * 16
                            acc_mm(kv, VWs[:, a, 64 * kv:64 * kv + 64], PTm[:, o_:o_ + 16], False, ["VWs", kPTm])
                            acc_mm(2 + kv, ones_b[:, 0:64], PTm[:, o_:o_ + 16], False, ["ones_b", kPTm])
                    new_tokens(KWNT, "KWNT", VWN, "VWN")
                    finish_branch(2)
                    if DBG_BAR:
                        P.flush()
                    acopy(totb[0:64, :], tot[0:64].rearrange("p k c -> p (k c)"), r=["tot"], w=["totb"])
                    pdup, kpdup = PXs.next()
                    mm(pdup[:, 0:32], II[0:64, :], totb[0:64, :], True, True, r=["II", "totb"], w=[kpdup])
                    dupv = pdup[:, 0:32].rearrange("p (k h q t) -> p k h q t", k=2, h=2, q=2)
                    for par in range(2):
                        pr = slice(64 * par, 64 * par + 64)
                        acopy(OBTs[pr, :, qcols].rearrange("p (k h) t -> p k h t", k=2), dupv[pr, :, :, par, :], r=[kpdup], w=["OBTs"])
                P.flush()
        H2T = BIG[:].rearrange("p a (c t) -> p (a c) t", c=4)
        NG = (DBG_NOWN + 3) // 4
        with contextlib.ExitStack() as SD:
            sbd = lambda n, s, d: SD.enter_context(sbuf(n, s, d))
            wg = sbd("wg", [128, 8, 2048], BF16)
            wbr = sbd("wbr", [128, 8, D], BF16)
            wo = sbd("wo", [128, 8, D], BF16)
            g2b = sbd("g2b", [128, D], F32)
            hTg = sbd("hTg", [128, 8, 512], BF16)
            SG = sbd("SG", [128, 16, 512], BF16)
            mT = sbd("mT", [128, 8, 512], BF16)
            RT5 = Ring(SD, sbuf, "t5", [128, 512], F32, 2)
            RT6 = Ring(SD, sbuf, "t6", [128, 512], F32, 2)
            RX2 = Ring(SD, sbuf, "x2t", [128, D], F32, 2)
            w_br_v = w_br.rearrange("(k p) c -> p k c", p=128)
            w_out_v = w_out.rearrange("(k p) c -> p k c", p=128)
            for k in range(8):
                load_cast(wg[:, k, :], w_in_v[:, k, C_MG:C_MG + 2048], "wg")
                load_cast(wbr[:, k, :], w_br_v[:, k, :], "wbr")
                load_cast(wo[:, k, :], w_out_v[:, k, :], "wo")
            dma("sp", g2b[:], g2.partition_broadcast(128)[:, 0, :], w=["g2b"])
            with contextlib.ExitStack() as SPd:
                PTR = Ring(SPd, psum, "ptrd", [128, 8, 128], BF16, 1)
                PG = Ring(SPd, psum, "pg", [128, 512], F32, 2)
                PYA = Ring(SPd, psum, "pya", [128, 512], F32, 1)
                PYB = Ring(SPd, psum, "pyb", [128, 512], F32, 1)
                PX = Ring(SPd, psum, "px", [128, 512], F32, 2)
                for G in range(NG + 1):
                    smp = (G == NG)
                    nt_g = 1 if smp else min(4, DBG_NOWN - 4 * G)
                    tn = 16 if smp else 128
                    W_ = tn * nt_g
                    gt = slice(0, 16) if smp else slice(512 * G, 512 * G + W_)
                    kB = "BIGs" if smp else "BIG%d" % G
                    oat_g = OATs if smp else OAT
                    obt_g = OBTs if smp else OBT
                    h2t_g = H2Ts if smp else H2T
                    for tl in range(nt_g):
                        tok = slice(0, 16) if smp else slice(512 * G + 128 * tl, 512 * G + 128 * tl + 128)
                        xt, kx = RX.next()
                        dma("sp", xt[:tn], xs if smp else xo[tok, :], w=[kx])
                        norm_T(xt, kx, g1b, "g1b", tn, hTg[:, :, tn * tl:tn * tl + tn], "hTg", PTR)
                    for fc in range(16):
                        pg, kpg = PG.next()
                        for k in range(8):
                            mm(pg[:, 0:W_], wg[:, k, 128 * fc:128 * fc + 128], hTg[:, k, 0:W_], k == 0, k == 7, r=["wg", "hTg"], w=[kpg])
                        act(SG[:, fc, 0:W_], pg[:, 0:W_], AF.Sigmoid, r=[kpg], w=["SG"])
                    for oc in range(8):
                        pya, kpa = PYA.next()
                        pyb, kpb = PYB.next()
                        for kc in range(4):
                            mm(pya[:, 0:W_], wbr[:, kc, 128 * oc:128 * oc + 128], oat_g[:, kc, gt], kc == 0, kc == 3, r=["wbr", "OAT", "OATs", kB], w=[kpa])
                        for kc in range(4):
                            mm(pyb[:, 0:W_], wbr[:, 4 + kc, 128 * oc:128 * oc + 128], obt_g[:, kc, gt], kc == 0, kc == 3, r=["wbr", "OBT", "OBTs", kB], w=[kpb])
                        t5, k5 = RT5.next()
                        t6, k6 = RT6.next()
                        tt(t5[:, 0:W_], pya[:, 0:W_], SG[:, oc, 0:W_], ALU.mult, r=[kpa, "SG"], w=[k5])
                        tt(t6[:, 0:W_], pyb[:, 0:W_], SG[:, 8 + oc, 0:W_], ALU.mult, r=[kpb, "SG"], w=[k6])
                        tt(mT[:, oc, 0:W_], t5[:, 0:W_], t6[:, 0:W_], ALU.add, r=[k5, k6], w=["mT"])
                    for tl in range(nt_g):
                        tok = slice(0, 16) if smp else slice(512 * G + 128 * tl, 512 * G + 128 * tl + 128)
                        xt, kx = RX.next()
                        dma("sp", xt[:tn], xs if smp else xo[tok, :], w=[kx])
                        x2t, kx2 = RX2.next()
                        for half in range(2):
                            px, kpx = PX.next()
                            for k in range(8):
                                mm(px[:tn, :], mT[:, k, tn * tl:tn * tl + tn], wo[:, k, 512 * half:512 * half + 512], k == 0, k == 7,
                                   r=["mT", "wo"], w=[kpx])
                            tt(x2t[:tn, 512 * half:512 * half + 512], px[:tn, :], xt[:tn, 512 * half:512 * half + 512], ALU.add, r=[kpx, kx], w=[kx2])
                        ydst = o_ys if smp else o_yp[tok, :]
                        dma("sp", ydst, x2t[:tn], r=[kx2], w=["ydram%s" % ("s" if smp else 4 * G + tl)])
                        norm_T(x2t, kx2, g2b, "g2b", tn, h2t_g[:, :, tok], kB, PTR)
                P.flush()
        with contextlib.ExitStack() as SE:
            sbe = lambda n, s, d: SE.enter_context(sbuf(n, s, d))
            wup = sbe("wup", [128, 8, 4096], BF16)
            wdn = sbe("wdn", [128, 32, D], BF16)
            RF = Ring(SE, sbuf, "fT", [128, 8, 256], BF16, 2)
            RR = Ring(SE, sbuf, "rl", [128, 256], F32, 2)
            w_up_v = w_up.rearrange("(k p) c -> p k c", p=128)
            w_dn_v = w_dn.rearrange("(k p) c -> p k c", p=128)
            for k in range(8):
                load_cast(wup[:, k, :], w_up_v[:, k, :], "wup")
            for k in range(32):
                load_cast(wdn[:, k, :], w_dn_v[:, k, :], "wdn")
            with contextlib.ExitStack() as SPe:
                PY = Ring(SPe, psum, "py", [128, 1024], F32, 2)
                PU = Ring(SPe, psum, "pu", [128, 256], F32, 4)
                NG2 = (DBG_NOWN + 1) // 2
                for G2 in range(NG2 + 1):
                    smp = (G2 == NG2)
                    nt_g = 1 if smp else min(2, DBG_NOWN - 2 * G2)
                    tn = 16 if smp else 128
                    W_ = tn * nt_g
                    gt = slice(0, 16) if smp else slice(256 * G2, 256 * G2 + W_)
                    h2t_g = H2Ts if smp else H2T
                    pys = [PY.next() for _ in range(nt_g)]
                    for q in range(4):
                        fT, kf = RF.next()
                        for fc in range(8):
                            pu, kpu = PU.next()
                            for k in range(8):
                                mm(pu[:, 0:W_], wup[:, k, 1024 * q + 128 * fc:1024 * q + 128 * fc + 128], h2t_g[:, k, gt], k == 0, k == 7,
                                   r=["wup", "H2T"], w=[kpu])
                            rl, krl = RR.next()
                            act(rl[:, 0:W_], pu[:, 0:W_], AF.Relu, r=[kpu], w=[krl])
                            tt(fT[:, fc, 0:W_], rl[:, 0:W_], rl[:, 0:W_], ALU.mult, r=[krl], w=[kf])
                        for tl in range(nt_g):
                            py, kpy = pys[tl]
                            for half in range(2):
                                for fc in range(8):
                                    mm(py[:tn, 512 * half:512 * half + 512], fT[:, fc, tn * tl:tn * tl + tn],
                                       wdn[:, 8 * q + fc, 512 * half:512 * half + 512], q == 0 and fc == 0, q == 3 and fc == 7,
                                       r=[kf, "wdn"], w=[kpy])
                    for tl in range(nt_g):
                        ti = "s" if smp else 2 * G2 + tl
                        ydst = o_ys if smp else o_yp[128 * ti:128 * ti + 128, :]
                        py, kpy = pys[tl]
                        xt, kx = RX.next()
                        dma("sp", xt[:tn], ydst, r=["ydram%s" % ti], w=[kx])
                        tt(xt[:tn], xt[:tn], py[:tn, :], ALU.add, r=[kx, kpy], w=[kx])
                        dma("sp", ydst, xt[:tn], r=[kx], w=["ydram%s" % ti], o=True)
                P.flush()
        P.finish()
    return nc


def _rope_tables(pos):
    inv = (500000.0 ** (-np.arange(0, 16, 2, dtype=np.float32) / 16.0)).astype(np.float32)
    ang = pos.astype(np.float32)[:, None] * inv[None, :]
    return np.cos(ang).astype(np.float32), np.sin(ang).astype(np.float32)


_NC_CACHE = {}


def kernel(**inp):
    f32 = lambda a: np.ascontiguousarray(np.asarray(a), dtype=np.float32)
    x_prompt = f32(inp["x_prompt"]); x_sample = f32(inp["x_sample"])
    if "nc" not in _NC_CACHE:
        _NC_CACHE["nc"] = build()
    nc = _NC_CACHE["nc"]
    pos = np.arange(SEQ)
    cA, sA = _rope_tables(pos)
    cosA = np.ascontiguousarray(cA.reshape(NT, 128, 8).transpose(1, 0, 2))
    sinA = np.ascontiguousarray(sA.reshape(NT, 128, 8).transpose(1, 0, 2))
    eind = np.zeros((64, SEQ), np.float32)
    blk = (np.arange(SEQ) // 64) % 64
    eind[blk, np.arange(SEQ)] = 1.0
    zw = np.zeros((32, 288), np.float32)
    zw[np.arange(32), np.arange(32) + 128] = 1.0
    tril = np.tril(np.ones((128, 128), np.float32))
    shared = dict(
        g1=f32(inp["g_norm1"]), w_in=f32(inp["w_in"][0]),
        g_ks=f32(inp["g_ks"]), g_kw=f32(inp["g_kw"]), g_q=f32(inp["g_q"]), g_kc=f32(inp["g_kc"]),
        w_ck1=f32(inp["w_ck1"][0]), w_ck2=f32(inp["w_ck2"][0]), pe_k=f32(inp["pe_k"][0]),
        w_cv1=f32(inp["w_cv1"][0]), w_cv2=f32(inp["w_cv2"][0]), pe_v=f32(inp["pe_v"][0]),
        ident=np.eye(128, dtype=np.float32), cosA=cosA, sinA=sinA, eind=eind,
        lng=f32(inp["ln_v_g"]), lnb=f32(inp["ln_v_b"]), w_s=f32(inp["w_s"][0]), b_s=f32(inp["b_s"][0]),
        w_br=f32(inp["w_branch"][0]), w_out=f32(inp["w_out"][0]), g2=f32(inp["g_norm2"]),
        w_up=f32(inp["w_up"][0]), w_dn=f32(inp["w_down"][0]), zw=zw, tril=tril,
    )
    sidx = np.arange(128)[:, None]
    qidx = np.arange(128)[None, :]
    if DBG_SAMPLE:
        oh = np.zeros((24, 24, 128), np.float32)
        oh[np.arange(24), np.arange(24), :] = 1.0
        cm = np.ones((128, 8), np.float32); cm[127, 7] = 0.0
        mw4 = np.ones((128, 4, 16), np.float32)
        pp_ = np.arange(128)[:, None]; tt_ = (np.arange(16) % 4)[None, :]
        mw4[:, 0, :] = (pp_ > tt_).astype(np.float32)
        mnew = np.zeros((128, 4, 16), np.float32)
        for kp in range(16):
            for sq in range(4):
                for col in range(16):
                    if kp // 4 == sq and (kp % 4) <= (col % 4):
                        mnew[kp, sq, col] = 1.0
        pools = [np.ascontiguousarray(np.asarray(inp[k], dtype=np.float32).reshape(-1, 16384)[:DBG_NPAGES])
                 for k in ("cache_k_cmp", "cache_v_cmp", "cache_k_sel", "cache_v_sel")]
        shared.update(oh=oh, cmask=cm, mw4=mw4, mnew=mnew, pkc=pools[0], pvc=pools[1], pks=pools[2], pvs=pools[3])
        ptab = np.asarray(inp["page_table"]).astype(np.int32)
        if DBG_NPAGES < 5120:
            ptab = ptab % DBG_NPAGES
        kwin_all = f32(inp["cache_k_win"][0]).reshape(32, 512, 128)
        vwin_all = f32(inp["cache_v_win"][0]).reshape(32, 512, 128)
    in_maps = []
    for c in range(8):
        b, j = c // 4, c % 4
        m = dict(shared)
        if DBG_SAMPLE:
            m["pt"] = np.ascontiguousarray(ptab[4 * c:4 * c + 4].T)
            m["kwin"] = np.ascontiguousarray(kwin_all[4 * c:4 * c + 4])
            m["vwin"] = np.ascontiguousarray(vwin_all[4 * c:4 * c + 4])
        m["xb"] = x_prompt[b]
        own = x_prompt[b].reshape(NT, 128, D)[j::4]
        m["xo"] = np.ascontiguousarray(own.reshape(NOWN * 128, D))
        m["xs"] = np.ascontiguousarray(x_sample[4 * c:4 * c + 4].reshape(16, D))
        posO = np.zeros((NOWN + 1, 128), np.int64)
        for i in range(NOWN):
            posO[i] = 128 * (4 * i + j) + np.arange(128)
        posO[NOWN, :16] = PAST + (np.arange(16) % 4)
        cO, sO = _rope_tables(posO.reshape(-1))
        m["cosO"] = np.ascontiguousarray(cO.reshape(NOWN + 1, 128, 8).transpose(1, 0, 2))
        m["sinO"] = np.ascontiguousarray(sO.reshape(NOWN + 1, 128, 8).transpose(1, 0, 2))
        ms = np.zeros((128, 4, 128), np.float32)
        for mm_ in range(4):
            d = j - mm_
            if d < 0:
                ms[:, mm_, :] = NEGM
            elif d == 0:
                ms[:, mm_, :] = np.where(sidx > qidx, NEGM, 0.0)
        m["msel"] = ms
        mw = np.zeros((128, 8, 128), np.float32)
        for mm_ in range(8):
            d = j + 4 - mm_
            if d < 0 or d > 4:
                mw[:, mm_, :] = NEGM
            elif d == 0:
                mw[:, mm_, :] = np.where(sidx > qidx, NEGM, 0.0)
            elif d == 4:
                mw[:, mm_, :] = np.where(sidx <= qidx, NEGM, 0.0)
        m["mwin"] = mw
        nn = np.arange(32)[None, :]
        qq = np.arange(128)[:, None]
        mcq = np.where(16 * nn + 15 <= 128 * j + qq, 0.0, NEGM).astype(np.float32)
        m["mcq"] = mcq
        m["mct"] = np.ascontiguousarray(mcq.T)
        col = np.arange(256)[None, :] - 128
        cur = 2 * j + (qq // 64)
        m["m1c"] = (col < cur - 1).astype(np.float32)
        m2 = np.zeros((128, 256), np.float32)
        m2[(col == cur - 1) | (col == cur)] = 1e4
        m2[col > cur] = -1e4
        m["m2c"] = m2
        in_maps.append(m)
    res = run_bass_kernel_spmd(nc, in_maps, core_ids=list(range(8)))
    R = res.results
    y_p = np.zeros((2, SEQ, D), np.float32)
    for c in range(8):
        b, j = c // 4, c % 4
        y_p[b].reshape(NT, 128, D)[j::4] = R[c]["o_yp"].reshape(NOWN, 128, D)
    y_s = np.zeros((32, 4, D), np.float32)
    pk = np.stack([R[4 * b]["o_pkv"] for b in range(2)])
    p_kc, p_vc, p_ks, p_vs = [np.ascontiguousarray(pk[:, :, i * 128:(i + 1) * 128]).reshape(1, 2, SEQ, 2, 64) for i in range(4)]
    pw = np.stack([R[4 * b]["o_pw"] for b in range(2)])
    p_kw, p_vw = [np.ascontiguousarray(pw[:, :, i * 128:(i + 1) * 128]).reshape(1, 2, 512, 2, 64) for i in range(2)]
    p_va = np.stack([R[4 * b + 3]["o_pva"] for b in range(2)])[None]
    skv = np.concatenate([R[c]["o_skv"] for c in range(8)], axis=0)
    s6 = [np.ascontiguousarray(skv[:, i * 128:(i + 1) * 128]).reshape(1, 32, 4, 2, 64) for i in range(6)]
    s_va = np.concatenate([R[c]["o_sva"] for c in range(8)], axis=0).reshape(1, 32, 4, 512)
    y_s = np.concatenate([R[c]["o_ys"] for c in range(8)], axis=0).reshape(32, 4, D)
    return (y_p, y_s, p_kc, p_vc, p_ks, p_vs, p_kw, p_vw, p_va, *s6, s_va)
```
